# Optimizing a Trainium2 kernel written in Bass

```python
import jax, jax.numpy as jnp
from jax import lax
import numpy as np

D_MODEL = 1024
BATCH = 8
SEQ = 2048
DEPTH = 1

D_MIX = D_MODEL
HEAD_DIM = 64
ATTN_WIDTH = D_MIX // 2
ATTN_HEADS = ATTN_WIDTH // HEAD_DIM
CONV_DIM = D_MIX - ATTN_WIDTH
CONV_GROUPS = 8
CONV_K = 3
D_FF = 2816
BLOCK_Q = 128
EPS = 1e-6
N_MOD = 6
IN_SPLITS = (ATTN_WIDTH, 2 * ATTN_WIDTH, 3 * ATTN_WIDTH, 3 * ATTN_WIDTH + ATTN_HEADS,
             3 * ATTN_WIDTH + ATTN_HEADS + CONV_DIM, 3 * ATTN_WIDTH + ATTN_HEADS + 2 * CONV_DIM)
D_IN = 3 * ATTN_WIDTH + ATTN_HEADS + 3 * CONV_DIM

kernel_name = "hymba_fox_shortconv_convffn_adaln"


def rmsnorm(x, g):
    xf = x.astype(jnp.float32)
    xf = xf * lax.rsqrt(jnp.mean(xf * xf, axis=-1, keepdims=True) + EPS)
    return (xf * g.astype(jnp.float32)).astype(x.dtype)


def causal_dwconv(u, w):
    s = u.shape[1]
    up = jnp.pad(u, ((0, 0), (CONV_K - 1, 0), (0, 0)))
    y = w[0] * up[:, 0:s]
    for k in range(1, CONV_K):
        y = y + w[k] * up[:, k:k + s]
    return y


def forgetting_attention(q, k, v, logf):
    b, s, h, dh = q.shape
    fcum = jnp.cumsum(logf, axis=1)
    fcum = jnp.transpose(fcum, (0, 2, 1))
    scale = 1.0 / np.sqrt(dh)
    outs = []
    for i in range(s // BLOCK_Q):
        q0, q1 = i * BLOCK_Q, (i + 1) * BLOCK_Q
        qb = q[:, q0:q1]
        kb = k[:, :q1]
        vb = v[:, :q1]
        logits = jnp.einsum('bqhd,bkhd->bhqk', qb, kb).astype(jnp.float32) * scale
        logits = logits + fcum[:, :, q0:q1, None] - fcum[:, :, None, :q1]
        q_pos = q0 + jnp.arange(BLOCK_Q)
        k_pos = jnp.arange(q1)
        causal = k_pos[None, :] <= q_pos[:, None]
        logits = jnp.where(causal[None, None], logits, -jnp.inf)
        p = jax.nn.softmax(logits, axis=-1).astype(v.dtype)
        outs.append(jnp.einsum('bhqk,bkhd->bqhd', p, vb))
    return jnp.concatenate(outs, axis=1)


def hybrid_layer(x, c_act, w_ada, b_ada, norm1_g, w_in, b_forget, q_norm_g, k_norm_g,
                 conv_mix_w, w_out, norm2_g, w_up, ffn_conv_w, w_down):
    b, s, _ = x.shape
    mod = (c_act @ w_ada + b_ada)[:, None, :]
    sh1, sc1, g1, sh2, sc2, g2 = jnp.split(mod, N_MOD, axis=-1)

    h = rmsnorm(x, norm1_g) * (1 + sc1) + sh1
    proj = h @ w_in
    q, k, v, fg, xin, bg, cg = jnp.split(proj, IN_SPLITS, axis=-1)
    q = rmsnorm(q.reshape(b, s, ATTN_HEADS, HEAD_DIM), q_norm_g)
    k = rmsnorm(k.reshape(b, s, ATTN_HEADS, HEAD_DIM), k_norm_g)
    v = v.reshape(b, s, ATTN_HEADS, HEAD_DIM)
    logf = jax.nn.log_sigmoid((fg + b_forget).astype(jnp.float32))
    attn = forgetting_attention(q, k, v, logf).reshape(b, s, ATTN_WIDTH)

    conv = bg * causal_dwconv(cg * xin, conv_mix_w)

    mixed = jnp.concatenate([attn, conv], axis=-1)
    x = x + g1 * (mixed @ w_out)

    h2 = rmsnorm(x, norm2_g) * (1 + sc2) + sh2
    u = causal_dwconv(h2 @ w_up, ffn_conv_w)
    u_gate, u_val = jnp.split(u, 2, axis=-1)
    y = (jax.nn.silu(u_gate) * u_val) @ w_down
    return x + g2 * y


def setup_inputs(seed: int = 0) -> dict:
    key = jax.random.key(seed)
    ks = jax.random.split(key, 16)
    f32 = jnp.float32
    nrm = lambda k, shape, std: jax.random.normal(k, shape, f32) * std
    return {
        "x": nrm(ks[0], (BATCH, SEQ, D_MODEL), 1.0),
        "c": nrm(ks[1], (BATCH, D_MODEL), 1.0),
        "w_ada": nrm(ks[2], (DEPTH, D_MODEL, N_MOD * D_MODEL), 0.5 * D_MODEL ** -0.5),
        "b_ada": nrm(ks[3], (DEPTH, N_MOD * D_MODEL), 0.02),
        "norm1_g": 1.0 + nrm(ks[4], (DEPTH, D_MODEL), 0.02),
        "w_in": nrm(ks[5], (DEPTH, D_MODEL, D_IN), D_MODEL ** -0.5),
        "b_forget": jax.random.uniform(ks[6], (DEPTH, ATTN_HEADS), f32, 1.0, 4.0),
        "q_norm_g": 1.0 + nrm(ks[7], (DEPTH, HEAD_DIM), 0.02),
        "k_norm_g": 1.0 + nrm(ks[8], (DEPTH, HEAD_DIM), 0.02),
        "conv_mix_w": nrm(ks[9], (DEPTH, CONV_K, CONV_DIM), CONV_K ** -0.5),
        "w_out": nrm(ks[10], (DEPTH, D_MIX, D_MODEL), D_MIX ** -0.5),
        "norm2_g": 1.0 + nrm(ks[11], (DEPTH, D_MODEL), 0.02),
        "w_up": nrm(ks[12], (DEPTH, D_MODEL, 2 * D_FF), D_MODEL ** -0.5),
        "ffn_conv_w": nrm(ks[13], (DEPTH, CONV_K, 2 * D_FF), CONV_K ** -0.5),
        "w_down": nrm(ks[14], (DEPTH, D_FF, D_MODEL), D_FF ** -0.5),
    }


def reference(x, c, w_ada, b_ada, norm1_g, w_in, b_forget, q_norm_g, k_norm_g,
              conv_mix_w, w_out, norm2_g, w_up, ffn_conv_w, w_down):
    c_act = jax.nn.silu(c)
    for l in range(DEPTH):
        x = hybrid_layer(x, c_act, w_ada[l], b_ada[l], norm1_g[l], w_in[l], b_forget[l],
                         q_norm_g[l], k_norm_g[l], conv_mix_w[l], w_out[l], norm2_g[l],
                         w_up[l], ffn_conv_w[l], w_down[l])
    return x
```

```python
import numpy as np
import ml_dtypes
import concourse.bass as bass
import concourse.mybir as mybir
from concourse.bass_utils import run_bass_kernel_spmd

F32 = mybir.dt.float32
BF16 = mybir.dt.bfloat16
AF = mybir.ActivationFunctionType
ALU = mybir.AluOpType

S_LEN = 2048
D = 1024
NT = 16
D_IN = 3080
D_FF = 2816
NJ = 22
EPS = 1e-6
DEBUG = False
STOP = 99
NOFOLD = False
NONORM = False


class Reg:
    __slots__ = ("space", "lo", "hi")

    def __init__(self, space, lo, hi):
        self.space, self.lo, self.hi = space, lo, hi


class V:
    __slots__ = ("ap", "regs")

    def __init__(self, ap, regs):
        self.ap = ap
        self.regs = regs

    def __getitem__(self, k):
        return V(self.ap[k], self.regs)


class Op:
    __slots__ = ("eng", "fn", "deps", "sig", "seq", "dkey", "dwaits")

    def __init__(self, eng, fn):
        self.eng, self.fn = eng, fn
        self.deps = []
        self.sig = False
        self.seq = 0
        self.dkey = None
        self.dwaits = {}


ENGS = ["pe", "act", "dve", "pool", "sp"]


class Sched:
    def __init__(self):
        self.ops = {e: [] for e in ENGS}
        self.acc = {}
        self.dma_cum = {}
        self.nops = 0

    def _access(self, op, rr, ww):
        deps = {}
        for r in rr:
            tab = self.acc.setdefault(r.space, {})
            for (lo, hi, eng, isw, uid), o in tab.items():
                if isw and lo < r.hi and r.lo < hi:
                    deps[id(o)] = o
        for r in ww:
            tab = self.acc.setdefault(r.space, {})
            for (lo, hi, eng, isw, uid), o in tab.items():
                if lo < r.hi and r.lo < hi:
                    deps[id(o)] = o
        for r in ww:
            tab = self.acc[r.space]
            for k in [k for k in tab if r.lo <= k[0] and k[1] <= r.hi]:
                del tab[k]
        self.nops += 1
        uid = self.nops if op.dkey is not None else 0
        for r in ww:
            self.acc[r.space][(r.lo, r.hi, op.eng, True, uid)] = op
        for r in rr:
            self.acc[r.space][(r.lo, r.hi, op.eng, False, uid)] = op
        deps.pop(id(op), None)
        return list(deps.values())

    def add(self, eng, fn, R=(), W=(), dma=None):
        op = Op(eng, fn)
        op.dkey = dma
        rr = [g for v in R for g in v.regs]
        ww = [g for v in W for g in v.regs]
        op.deps = self._access(op, rr, ww)
        for d in op.deps:
            if d.dkey is not None:
                op.dwaits[d.dkey] = self.dma_cum[d.dkey]
        if dma is not None:
            self.dma_cum[dma] = self.dma_cum.get(dma, 0) + 16
        self.ops[eng].append(op)
        return op

    def emit(self, block, sems, dsems):
        for e in ENGS:
            for op in self.ops[e]:
                for d in op.deps:
                    if d.dkey is not None or (d.eng == "pe" and op.eng == "pe"):
                        continue
                    d.sig = True
        for e in ENGS:
            n = 0
            for op in self.ops[e]:
                if op.sig:
                    n += 1
                    op.seq = n

        def body(e):
            def run(eng):
                waited = {}
                for op in self.ops[e]:
                    need = {}
                    for d in op.deps:
                        if d.dkey is not None or (d.eng == "pe" and op.eng == "pe"):
                            continue
                        need[d.eng] = max(need.get(d.eng, 0), d.seq)
                    for pe_, v in need.items():
                        if waited.get(("e", pe_), 0) < v:
                            eng.wait_ge(sems[pe_], v)
                            waited[("e", pe_)] = v
                    for k, v in op.dwaits.items():
                        if waited.get(("d", k), 0) < v:
                            eng.wait_ge(dsems[k], v)
                            waited[("d", k)] = v
                    if op.fn is None:
                        continue
                    ins = op.fn(eng)
                    if op.dkey is not None:
                        ins.then_inc(dsems[op.dkey], 16)
                    elif op.sig:
                        ins.then_inc(sems[e], 1)
            return run

        block.tensor(body("pe"))
        block.scalar(body("act"))
        block.vector(body("dve"))
        block.gpsimd(body("pool"))
        block.sync(body("sp"))


DMA_KEYS = ["small", "x0", "x1", "wada0", "wada1", "win0", "win1", "win2", "win3", "win4", "wfg", "modd", "gate", "wst0", "wst1",
            "up0", "up1", "up2", "x1st", "outst", "dbg", "x2", "x3", "wada2", "wada3", "small2", "wdn", "gate2", "wadaL0", "wadaL1", "upf"]

ARENA_BYTES = 207872


def build_nc():
    nc = bass.Bass("TRN2", target_bir_lowering=False)
    Dm = {}

    def din(name, shape, dt=F32):
        Dm[name] = nc.dram_tensor(name, shape, dt, kind="ExternalInput").ap()

    din("x", [S_LEN, D]); din("c", [128, 8]); din("w_ada", [D, 6 * D]); din("b_fm", [128, 48])
    din("b_ada", [1, 6 * D]); din("g1n", [128, 8]); din("g2n", [128, 8])
    din("w_in", [D, D_IN]); din("bfg", [104, 1]); din("gq", [128, 1]); din("gk", [128, 1])
    din("cw", [128, 12]); din("w_out", [D, D]); din("w_up", [D, 2 * D_FF]); din("fcw", [128, 132])
    din("w_down", [D_FF, D])
    din("k_ident", [128, 128], BF16); din("k_bones", [128, 128], BF16); din("k_mask", [128, 128], BF16)
    din("k_sel", [128, 1024], BF16); din("k_identf", [8, 8]); din("k_ones", [1, 2048], BF16)
    Dm["out"] = nc.dram_tensor("out", [S_LEN, D], F32, kind="ExternalOutput").ap()
    Dm["x1d"] = nc.dram_tensor("x1d", [S_LEN, D], F32).ap()
    Dm["modd"] = nc.dram_tensor("modd", [1, 6 * D], F32).ap()
    dbg_specs = {}
    if DEBUG:
        for nm, shp, dt in [("dbg_hT", [128, 8, 512], BF16), ("dbg_qs", [128, 4, 2048], BF16),
                            ("dbg_kn", [128, 4, 2048], BF16), ("dbg_vext", [128, 16, 1024], BF16),
                            ("dbg_frows", [64, 2048], BF16), ("dbg_negf", [128, 128], F32),
                            ("dbg_convT", [128, 4, 2048], BF16), ("dbg_attnT", [128, 4, 2048], BF16),
                            ("dbg_h2T", [128, 8, 2048], BF16), ("dbg_modfm", [128, 48], F32)]:
            Dm[nm] = nc.dram_tensor(nm, shp, dt, kind="ExternalOutput").ap()

    def dv(name, ap=None, lo=0, hi=1):
        return V(Dm[name] if ap is None else ap, [Reg("d_" + name, lo, hi)])

    with (
        nc.sbuf_tensor("arena", [128, ARENA_BYTES // 4], F32) as at,
        nc.psum_tensor("ps", [128, 4096], F32) as pst,
        nc.semaphore("s_pe") as s_pe, nc.semaphore("s_act") as s_act, nc.semaphore("s_dve") as s_dve,
        nc.semaphore("s_pool") as s_pool, nc.semaphore("s_sp") as s_sp,
        nc.Block() as block,
    ):
        dsem_handles = {k: nc.alloc_semaphore(name="dq_" + k) for k in DMA_KEYS}
        S = Sched()

        def view(off, shape, dt):
            esz = 2 if dt == BF16 else 4
            nb = int(np.prod(shape[1:])) * esz
            assert off % 4 == 0 and nb % 4 == 0 and off + nb <= ARENA_BYTES, (off, nb)
            ap = at[0:shape[0], off // 4:(off + nb) // 4]
            if dt != F32:
                ap = ap.bitcast(dt)
            if len(shape) == 3:
                ap = ap.rearrange("p (a b) -> p a b", b=shape[2])
            return V(ap, [Reg("sb", off, off + nb)])

        bank_ctr = [0]
        rot = [[0, 1, 2, 3, 4, 5, 6]]

        def bank(b=None, n=1):
            if b is None:
                b = rot[0][bank_ctr[0] % len(rot[0])]
                bank_ctr[0] += 1
            return V(pst[:, b * 512:(b + n) * 512], [Reg("ps", b * 2048, (b + n) * 2048)])

        def bankbf(bk):
            return V(bk.ap.bitcast(BF16), bk.regs)

        O_WIN, O_HTC, O_QK, O_VEXT, O_CONVT, O_ATTNT = 0, 49280, 65664, 98432, 131200, 147584
        O_FROWS, O_PT, O_NEGF, O_X, O_XN, O_SQJ, O_SCR, O_SCRB, O_SMALL = (
            163968, 168064, 171136, 171648, 179840, 183936, 185984, 198320, 200368)
        WIN = [view(O_WIN + kc * 6160, [128, D_IN], BF16) for kc in range(8)]
        WDN = [view(O_WIN + j * 2048, [128, 1024], BF16) for j in range(NJ)]
        HTC = [view(O_HTC + s * 8192, [128, 8, 512], BF16) for s in range(2)]
        WOUTG = [view(O_HTC + kc * 2048, [128, 1024], BF16) for kc in range(8)]
        ACC = [view(O_HTC + s * 4096, [128, 1024], F32) for s in range(4)]
        QS = [view(O_QK + c * 4096, [128, 2048], BF16) for c in range(4)]
        KPAD = [view(O_QK + 16384 + h * 4096, [128, 2048], BF16) for h in range(8)]
        WINALL = view(O_WIN, [128, 8, D_IN], BF16)
        H2T = [view(O_QK + kc * 4096, [128, 2048], BF16) for kc in range(8)]
        VEXT = [view(O_WIN + t * 2048, [128, 8, 128], BF16) for t in range(NT)]
        VTOK = [view(O_VEXT + 16384 + t * 1024, [128, 8, 64], BF16) for t in range(NT)]
        QPAD = [view(O_VEXT + 16384 + i * 4096, [128, 2048], BF16) for i in range(4)]
        GT = [view(O_VEXT + j * 2048, [128, 1024], BF16) for j in range(NJ)]
        UB = [view(O_VEXT + 45056 + s * 4104, [128, 1026], F32) for s in range(4)]
        CONVT = [view(O_CONVT + j * 4096, [128, 2048], BF16) for j in range(4)]
        ATTNT = [view(O_ATTNT + c * 4096, [128, 2048], BF16) for c in range(4)]
        WADA = [view(O_ATTNT + s * 4096, [128, 8, 256], BF16) for s in range(4)]
        FSCR = [view(O_ATTNT + 8192 + s * 2048, [104, 512], F32) for s in range(3)]
        FHI2 = view(O_ATTNT + 8192 + 6144, [104, 512], BF16)
        FROWS = view(O_FROWS, [128, 2048], BF16)
        PT = [view(O_PT + s * 1024, [128, 512], BF16) for s in range(3)]
        ONESF = view(O_PT, [104, 512], F32)
        NEGF = view(O_NEGF, [128, 128], F32)
        X = [view(O_X + s * 4096, [128, 1024], F32) for s in range(2)]
        XN = [view(O_XN + s * 2048, [128, 1024], BF16) for s in range(2)]
        SQJ = view(O_SQJ, [128, 1024], BF16)
        UPAN = [view(O_XN + s * 4096, [128, 8, 256], BF16) for s in range(3)]
        SCR = [view(O_SCR + s * 2056, [128, 514], F32) for s in range(6)]
        GATEB = [view(O_SCR + s * 4096, [128, 1024], F32) for s in range(2)]
        SCRB = [view(O_SCRB + s * 1024, [128, 512], BF16) for s in range(2)]
        ROWB = [view(O_SCRB + s * 1024, [1, 256], F32) for s in range(2)]
        so = [O_SMALL]

        def small(shape, dt=F32):
            esz = 2 if dt == BF16 else 4
            nb = (int(np.prod(shape[1:])) * esz + 31) // 32 * 32
            v = view(so[0], shape, dt)
            so[0] += nb
            return v

        IDENT = small([128, 128], BF16); BONES = small([128, 128], BF16); MASK = small([128, 128], BF16)
        SEL = small([128, 1024], BF16); IDENTF = small([8, 8]); CVEC = small([128, 8]); CACT = small([128, 8], BF16)
        G1N = small([128, 8]); G2N = small([128, 8]); BFM = small([128, 48]); MODFM = small([128, 48])
        A1 = small([128, 8]); A2 = small([128, 8]); B1 = small([128, 8]); B2 = small([128, 8])
        CW = small([128, 12]); FCW = small([128, 132]); GQ = small([128, 1]); GK = small([128, 1]); GQK = small([128, 1])
        BFG = small([104, 1]); NBFG = small([104, 1]); EPSV = small([128, 1]); ONE1 = small([128, 1])
        SSQ = small([128, 16]); RSTD = small([128, 16]); SS2 = small([128, 16]); RSTD2 = small([128, 16])
        FCAR = small([104, 1]); ZCAR = small([128, 8]); UCAR = small([128, 88]); WFG = small([128, 8, 104], BF16)
        assert so[0] <= ARENA_BYTES, so[0]

        def sr(x):
            return ([x] if isinstance(x, V) else []), (x.ap if isinstance(x, V) else x)

        def mm(out, lhsT, rhs, st, sp):
            S.add("pe", lambda e: e.matmul(out.ap, lhsT.ap, rhs.ap, start=st, stop=sp), R=[lhsT, rhs], W=[out])

        def tr(out, in_, ident):
            S.add("pe", lambda e: e.transpose(out.ap, in_.ap, ident.ap), R=[in_, ident], W=[out])

        def act(out, in_, func, bias=None, scale=1.0, accum=None):
            rb, bv = sr(bias) if bias is not None else ([], None)
            rs, sv = sr(scale)
            kw = {}
            if bv is not None:
                kw["bias"] = bv
            if accum is not None:
                kw["accum_out"] = accum.ap
            S.add("act", lambda e: e.activation(out=out.ap, in_=in_.ap, func=func, scale=sv, **kw),
                  R=[in_] + rb + rs, W=[out] + ([accum] if accum is not None else []))

        def tt(eng, out, a, b, op):
            S.add(eng, lambda e: e.tensor_tensor(out=out.ap, in0=a.ap, in1=b.ap, op=op), R=[a, b], W=[out])

        def ts(eng, out, in0, s1, op0, s2=None, op1=None):
            r1, v1 = sr(s1)
            r2, v2 = sr(s2) if s2 is not None else ([], None)
            if op1 is None:
                S.add(eng, lambda e: e.tensor_scalar(out=out.ap, in0=in0.ap, scalar1=v1, scalar2=None, op0=op0),
                      R=[in0] + r1, W=[out])
            else:
                S.add(eng, lambda e: e.tensor_scalar(out=out.ap, in0=in0.ap, scalar1=v1, scalar2=v2, op0=op0, op1=op1),
                      R=[in0] + r1 + r2, W=[out])

        def stt(out, in0, scalar, in1, op0, op1):
            r1, v1 = sr(scalar)
            S.add("dve", lambda e: e.scalar_tensor_tensor(out=out.ap, in0=in0.ap, scalar=v1, in1=in1.ap, op0=op0, op1=op1),
                  R=[in0, in1] + r1, W=[out])

        def cp(eng, out, in_):
            S.add(eng, lambda e: e.tensor_copy(out=out.ap, in_=in_.ap), R=[in_], W=[out])

        def recip(out, in_):
            S.add("dve", lambda e: e.reciprocal(out=out.ap, in_=in_.ap), R=[in_], W=[out])

        def memset(eng, out, val):
            S.add(eng, lambda e: e.memset(out.ap, val), W=[out])

        def dma(q, out, in_, key):
            S.add(q, lambda e: e.dma_start(out=out.ap, in_=in_.ap), R=[in_], W=[out], dma=key)

        def xrows(name, tt_):
            return dv(name, Dm[name][tt_ * 128:(tt_ + 1) * 128, :], tt_, tt_ + 1)

        for dst, nm in [(IDENT, "k_ident"), (BONES, "k_bones"), (MASK, "k_mask"), (SEL, "k_sel"), (IDENTF, "k_identf"),
                        (CVEC, "c"), (G1N, "g1n"), (G2N, "g2n"), (BFM, "b_fm"), (CW, "cw"), (FCW, "fcw"),
                        (GQ, "gq"), (GK, "gk"), (BFG, "bfg")]:
            dma("sp", dst, dv(nm), "small")
        memset("dve", EPSV, EPS)
        memset("dve", ONE1, 1.0)
        memset("dve", SSQ, 0.0)
        memset("dve", SS2, 0.0)
        memset("dve", ZCAR, 0.0)
        memset("dve", FROWS, 0.0)
        memset("dve", ONESF, 1.0)
        memset("pool", WFG, 0.0)
        def kpad_init():
            for h in range(8):
                zr = 64 if h % 2 == 0 else 0
                memset("dve", KPAD[h][zr:zr + 64, :], 0.0)
            for h in range(8):
                zr = 64 if h % 2 == 0 else 0
                for o8 in (0, 32):
                    dma("sp", KPAD[h][zr + o8 + h:zr + o8 + h + 1, :], dv("k_ones"), "small2")
        ts("dve", NBFG, BFG, -1.0, ALU.mult)
        stt(GQK, GQ, 0.125, GK, ALU.mult, ALU.mult)
        act(CACT, CVEC, AF.Silu)
        xslot = {}

        def load_x(tt_, src="x"):
            s = tt_ % 2
            dma("sp", X[s], xrows(src, tt_), "x%d" % s)

        load_x(0); load_x(1)

        w_ada_v = Dm["w_ada"].rearrange("(p j) n -> p j n", j=8)
        panel_order = [4, 5, 6, 7, 0, 1, 2, 3] + list(range(8, 24))
        mod_ps = bank(7)
        n_issued = [0]

        WADAL = [view(O_X + s * 4096, [128, 8, 256], BF16) for s in range(2)]
        ROWBL = [view(O_SQJ + s * 1024, [1, 256], F32) for s in range(2)]
        late = {"on": False}

        def pslot(i):
            return i % 4 if i < 8 else i % 2

        def issue_panel(i):
            pn = panel_order[i]
            s = pslot(i)
            if i >= 8:
                dma("pool", WADAL[s], V(w_ada_v[:, :, pn * 256:(pn + 1) * 256], [Reg("d_w_ada", pn, pn + 1)]), "wadaL%d" % s)
                return
            dma("pool", WADA[s], V(w_ada_v[:, :, pn * 256:(pn + 1) * 256], [Reg("d_w_ada", pn, pn + 1)]), "wada%d" % s)

        w_in_v = Dm["w_in"].rearrange("(kc p) n -> p kc n", p=128)
        WIN_PANELS = [(0, 512), (512, 1024), (1024, 1544), (1544, 2312), (2312, 3080)]

        def wregs(c0, c1, kcs=range(8)):
            return [Reg("sb", O_WIN + kc * 6160 + c0 * 2, O_WIN + kc * 6160 + c1 * 2) for kc in kcs]

        def win_load(pi):
            if pi < len(WIN_PANELS):
                c0, c1 = WIN_PANELS[pi]
                dma("pool", V(WINALL.ap[:, :, c0:c1], wregs(c0, c1)), V(w_in_v[:, :, c0:c1], [Reg("d_w_in", pi, pi + 1)]), "win%d" % pi)

        def wv(kc, c0, c1):
            return V(WIN[kc].ap[:, c0:c1], wregs(c0, c1, [kc]))

        for i_ in range(4):
            issue_panel(i_)
        win_load(0)
        kpad_init()

        def mod_panels(lo, hi, part="ab"):
          for i in range(lo, hi):
            pn = panel_order[i]
            if part == "b":
                mod_part_b(i)
                continue
            s = pslot(i)
            rp = (V(pst[:, 7 * 512 + 256:8 * 512], [Reg("ps", 7 * 2048, 8 * 2048)]) if i >= 8 else bank())
            wsl = WADAL[s] if i >= 8 else WADA[s]
            ROWB_ = ROWBL if i >= 8 else ROWB
            for j in range(8):
                mm(rp[0:1, 0:256], CACT[:, j:j + 1], wsl[:, j, :], j == 0, j == 7)
            if i >= 8:
                cp("dve", ROWB_[i % 2], rp[0:1, 0:256])
            else:
                act(ROWB_[i % 2], rp[0:1, 0:256], AF.Copy)
            if i < 4:
                issue_panel(i + 4)
            elif 8 <= i and i + 2 < 24:
                issue_panel(i + 2)
            if i == 3:
                for pi_ in range(1, 5):
                    win_load(pi_)
            if i == 7:
                for o8 in (0, 32, 64, 96):
                    cp("pool", WFG[:, :, o8:o8 + 8], V(WINALL.ap[:, :, 1536:1544], wregs(1536, 1544)))
            if part == "a":
                continue
            mod_part_b(i)

        def mod_part_b(i):
            pn = panel_order[i]
            ROWB_ = ROWBL if i >= 8 else ROWB
            sect = pn // 4
            if sect in (2, 5):
                dma("sp", dv("modd", Dm["modd"][0:1, pn * 256:(pn + 1) * 256], pn, pn + 1), ROWB_[i % 2], "modd")
            else:
                for hh in range(2):
                    col = pn * 2 + hh
                    mm(mod_ps[:, col:col + 1], ROWB_[i % 2][0:1, hh * 128:(hh + 1) * 128], ONE1[0:1, 0:1], True, True)
            if i == 7:
                tt("dve", MODFM[:, 0:16], mod_ps[:, 0:16], BFM[:, 0:16], ALU.add)
                stt(A1, MODFM[:, 8:16], 1.0, G1N, ALU.add, ALU.mult)
                cp("dve", B1, MODFM[:, 0:8])
        def mod_finish():
            tt("dve", MODFM[:, 24:40], mod_ps[:, 24:40], BFM[:, 24:40], ALU.add)
            stt(A2, MODFM[:, 32:40], 1.0, G2N, ALU.add, ALU.mult)
            cp("dve", B2, MODFM[:, 24:32])
            rot[0] = [0, 1, 2, 3, 4, 5, 6, 7]

        mod_panels(0, 8)
        if STOP <= 1:
            mod_panels(8, 24)
            mod_finish()

        def norm_tile_a(tt_, SSv, RSv, xsrc=None):
            s = tt_ % 2
            xsrc = X[s] if xsrc is None else xsrc
            act(SQJ, xsrc, AF.Square, accum=SSv[:, tt_:tt_ + 1])
            act(RSv[:, tt_:tt_ + 1], SSv[:, tt_:tt_ + 1], AF.Ln, bias=EPSV, scale=1.0 / D)
            act(RSv[:, tt_:tt_ + 1], RSv[:, tt_:tt_ + 1], AF.Exp, scale=-0.5)
            ts("dve", XN[s], xsrc, RSv[:, tt_:tt_ + 1], ALU.mult)

        def norm_tile_b(tt_, Av, Bv, dst_fn, evac_eng="dve"):
            s = tt_ % 2
            pb = bank()
            pbb = bankbf(pb)
            if evac_eng == "act":
                pbb2 = bankbf(bank())
                for c in range(8):
                    src = (pbb if c % 2 == 0 else pbb2)[:, (c // 2) * 128:(c // 2 + 1) * 128]
                    tr(src, XN[s][:, c * 128:(c + 1) * 128], IDENT)
                for c in range(8):
                    src = (pbb if c % 2 == 0 else pbb2)[:, (c // 2) * 128:(c // 2 + 1) * 128]
                    if c % 2 == 0:
                        act(dst_fn(c), src, AF.Identity, bias=Bv[:, c:c + 1], scale=Av[:, c:c + 1])
                    else:
                        ts("dve", dst_fn(c), src, Av[:, c:c + 1], ALU.mult, Bv[:, c:c + 1], ALU.add)
                return
            for c in range(8):
                tr(pbb[:, c * 128:(c + 1) * 128], XN[s][:, c * 128:(c + 1) * 128], IDENT)
            for c in range(8):
                ts("dve", dst_fn(c), pbb[:, c * 128:(c + 1) * 128], Av[:, c:c + 1], ALU.mult, Bv[:, c:c + 1], ALU.add)

        def norm_tile(tt_, SSv, RSv, Av, Bv, dst_fn, xsrc=None, evac_eng="dve"):
            norm_tile_a(tt_, SSv, RSv, xsrc)
            norm_tile_b(tt_, Av, Bv, dst_fn, evac_eng)

        def s2_a(tt_):
            norm_tile_a(tt_, SSQ, RSTD)
            if tt_ + 2 < NT:
                load_x(tt_ + 2)

        def pipe_step(tt_):
            tc, t4 = divmod(tt_, 4)
            norm_tile_b(tt_, A1, B1, lambda c, t4=t4, tc=tc: HTC[tc % 2][:, c, t4 * 128:(t4 + 1) * 128])
            if tt_ + 1 < NT:
                s2_a(tt_ + 1)

        scr_ctr = [0]

        def scr():
            v = SCR[scr_ctr[0] % 6]
            scr_ctr[0] += 1
            return v

        def stage3(tc, between=None):
            between = between or (lambda k: None)
            H = HTC[tc % 2]
            cols = slice(tc * 512, (tc + 1) * 512)
            def qk_finish(nq, sqb, qc):
                P2 = bank()
                mm(P2, BONES, sqb, True, True)
                rs = scr()
                act(rs[:, 0:512], P2, AF.Ln, bias=EPSV, scale=1.0 / 64)
                act(rs[:, 0:512], rs[:, 0:512], AF.Exp, scale=-0.5)
                if nq < 4:
                    stt(QS[nq][:, cols], qc[:, 0:512], GQK, rs[:, 0:512], ALU.mult, ALU.mult)
                else:
                    hA = 2 * (nq - 4)
                    tt("dve", KPAD[hA][0:64, cols], qc[0:64, 0:512], rs[0:64, 0:512], ALU.mult)
                    tt("dve", KPAD[hA + 1][64:128, cols], qc[64:128, 0:512], rs[64:128, 0:512], ALU.mult)

            pend = None
            for nq in range(8):
                P = bank()
                for kc in range(8):
                    mm(P, wv(kc, nq * 128, (nq + 1) * 128), H[:, kc, :], kc == 0, kc == 7)
                sqb = SCRB[nq % 2]
                act(sqb, P, AF.Square)
                qc = scr()
                act(qc[:, 0:512], P, AF.Copy)
                if pend is not None:
                    qk_finish(*pend)
                pend = (nq, sqb, qc)
                if nq == 3:
                    between(0)
                if nq == 7:
                    between(1)
            for t4 in range(4):
                tt_ = tc * 4 + t4
                P = bank()
                for kc in range(8):
                    mm(P, H[:, kc, t4 * 128:(t4 + 1) * 128], wv(kc, 1024, 1536), kc == 0, kc == 7)
                S.add("act", lambda e, P=P, tt_=tt_: e.activation(
                    out=VTOK[tt_].ap, in_=P.ap.rearrange("p (h d) -> p h d", d=64), func=AF.Copy),
                    R=[P], W=[VTOK[tt_]])
                if pend is not None:
                    qk_finish(*pend)
                    pend = None
            between(2)
            P = bank()
            for kc in range(8):
                mm(P[0:104, :], WFG[:, kc, :], H[:, kc, :], kc == 0, kc == 7)
            e_ = FSCR[0]; l_ = FSCR[1]; f_ = FSCR[2]
            act(e_, P[0:104, :], AF.Exp, bias=NBFG, scale=-1.0)
            act(l_, e_, AF.Ln, bias=ONE1[0:104, :], scale=1.0)
            init = 0.0 if tc == 0 else FCAR
            ri, iv = sr(init)
            S.add("dve", lambda e, iv=iv: e.tensor_tensor_scan(out=f_.ap, data0=ONESF.ap, data1=l_.ap, initial=iv,
                                                             op0=ALU.mult, op1=ALU.subtract),
                  R=[ONESF, l_] + ri, W=[f_])
            cp("dve", FCAR, f_[:, 511:512])
            for b0 in (0, 64):
                cp("dve", FROWS[b0:b0 + 8, cols], f_[b0:b0 + 8, :])
                cp("dve", FHI2[b0 + 32:b0 + 40, :], f_[b0 + 32:b0 + 40, :])
                tt("dve", FROWS[b0 + 32:b0 + 40, cols], f_[b0 + 32:b0 + 40, :], FHI2[b0 + 32:b0 + 40, :], ALU.subtract)
            for j in range(4):
                Px = bank(); Pb = bank(); Pc = bank()
                for (Pq, base) in ((Px, 1544), (Pb, 2056), (Pc, 2568)):
                    for kc in range(8):
                        mm(Pq, wv(kc, base + j * 128, base + (j + 1) * 128), H[:, kc, :], kc == 0, kc == 7)
                xs = scr()
                act(xs[:, 0:512], Px, AF.Copy)
                z = scr()
                tt("dve", z[:, 2:514], Pc, xs[:, 0:512], ALU.mult)
                cp("dve", z[:, 0:2], ZCAR[:, 2 * j:2 * j + 2])
                cp("dve", ZCAR[:, 2 * j:2 * j + 2], z[:, 512:514])
                a_ = scr()
                ts("dve", a_[:, 0:512], z[:, 0:512], CW[:, j * 3:j * 3 + 1], ALU.mult)
                stt(a_[:, 0:512], z[:, 1:513], CW[:, j * 3 + 1:j * 3 + 2], a_[:, 0:512], ALU.mult, ALU.add)
                stt(a_[:, 0:512], z[:, 2:514], CW[:, j * 3 + 2:j * 3 + 3], a_[:, 0:512], ALU.mult, ALU.add)
                tt("dve", CONVT[j][:, cols], Pb, a_[:, 0:512], ALU.mult)
                if j == 1:
                    between(3)
            Pt = bank()
            for t4 in range(4):
                tr(Pt[:, t4 * 8:(t4 + 1) * 8], f_[0:8, t4 * 128:(t4 + 1) * 128], IDENTF)
            ts("dve", NEGF[:, tc * 32:(tc + 1) * 32], Pt[:, 0:32], -1.0, ALU.mult)

        if STOP >= 2:
          s2_a(0)
          for t_ in range(4):
              pipe_step(t_)
          for tc in range(4):
            if STOP >= 3:
                stage3(tc, (lambda k, tc=tc: pipe_step(4 * (tc + 1) + k)) if tc + 1 < 4 else None)
            elif tc + 1 < 4:
                for k in range(4):
                    pipe_step(4 * (tc + 1) + k)
            pass
        for _ in range(0):
            pass
            if DEBUG and tc == 0:
                dma("sp", dv("dbg_hT"), HTC[0], "dbg")

        WSTG = [view(O_SCR + 4 * 2056, [128, 1024], F32), view(O_FROWS, [128, 1024], F32)]

        XNSTG = view(O_XN, [128, 1024], F32)

        def fold_tasks(wname, nchunks, dst, gate_sect):
            tasks = []

            def t_gate():
                dma("sp", GATEB[0], V(Dm["modd"][0:1, gate_sect * 1024:(gate_sect + 1) * 1024].partition_broadcast(128),
                                      [Reg("d_modd", gate_sect * 4, gate_sect * 4 + 4)]), "gate")
                dma("sp", GATEB[1], V(Dm["b_ada"][0:1, gate_sect * 1024:(gate_sect + 1) * 1024].partition_broadcast(128),
                                      [Reg("d_b_ada", 0, 1)]), "gate")
            tasks.append(t_gate)

            def t_first():
                tt("dve", GATEB[0], GATEB[0], GATEB[1], ALU.add)
                dma("sp", XNSTG, dv(wname, Dm[wname][0:128, :], 0, 1), "wst0")
            tasks.append(t_first)
            for kc in range(nchunks):
                def t_chunk(kc=kc):
                    tt("dve", dst[kc], XNSTG, GATEB[0], ALU.mult)
                    if kc + 1 < nchunks:
                        dma("sp", XNSTG, dv(wname, Dm[wname][(kc + 1) * 128:(kc + 2) * 128, :], kc + 1, kc + 2), "wst0")
                tasks.append(t_chunk)
            return tasks

        def attention():
            LOOK = 3
            steps = [(h, qc, kb) for h in range(8) for qc in range(4) for kb in range(4 * qc + 4)]
            sbank = {}
            state = {"A": None, "a_ctr": 0}

            def geom(h, qc, kb):
                d_ = kb - 4 * qc
                col0 = max(0, d_) * 128
                return d_, col0, qc * 512 + col0

            def emit_qk(i):
                h, qc, kb = steps[i]
                c = h // 2
                rows = slice((h % 2) * 64, (h % 2) * 64 + 64)
                d_, col0, q0 = geom(h, qc, kb)
                Sb = bank(3 + i % 4)
                sbank[i] = Sb
                if qc == 0 and kb == 0:
                    for hb in ([0, 1] if h == 0 else ([h + 1] if h + 1 < 8 else [])):
                        cb = hb // 2
                        dr_ = slice((hb % 2) * 64, (hb % 2) * 64 + 64)
                        fr_ = slice(64 - (hb % 2) * 64, 128 - (hb % 2) * 64)
                        cp("dve", QPAD[hb % 4][dr_, :], QS[cb][dr_, :])
                        cp("dve", QPAD[hb % 4][fr_, :], FROWS[fr_, :])
                mm(Sb[:, col0:512], KPAD[h][:, kb * 128:(kb + 1) * 128], QPAD[h % 4][:, q0:(qc + 1) * 512], True, d_ < 0)
                if d_ >= 0:
                    mm(Sb[:, col0:col0 + 128], IDENT, MASK, False, True)

            def emit_rest(i):
                h, qc, kb = steps[i]
                c = h // 2
                r0 = (h % 2) * 64
                nkb = 4 * qc + 4
                d_, col0, q0 = geom(h, qc, kb)
                if kb == 0:
                    state["A"] = bank(state["a_ctr"] % 3)
                    state["a_ctr"] += 1
                A = state["A"]
                Sb = sbank.pop(i)
                P_ = PT[i % 3]
                act(P_[:, col0:512], Sb[:, col0:512], AF.Exp, bias=NEGF[:, kb * 8 + h:kb * 8 + h + 1], scale=1.0)
                mm(A[:, col0:512], VEXT[kb][:, h, :], P_[:, col0:512], kb == 0, kb == nkb - 1)
                if kb == nkb - 1:
                    rd = SCR[4 + state["a_ctr"] % 2]
                    recip(rd[0:64, 0:512], A[64:128, :])
                    ocols = slice(qc * 512, (qc + 1) * 512)
                    tt("dve", ATTNT[c][r0:r0 + 64, ocols], A[0:64, :], rd[0:64, 0:512], ALU.mult)

            VEXTALL = view(O_WIN, [128, NT * 8, 128], BF16)
            VTOKALL = view(O_VEXT + 16384, [128, NT * 8, 64], BF16)
            memset("pool", VEXTALL[:, :, 64:128], 1.0)
            cp("dve", VEXTALL[:, :, 0:64], VTOKALL)
            n = len(steps)
            issue_panel(8); issue_panel(9)
            side = {}
            for p_ in range(8, 24):
                side.setdefault(12 + 17 * (p_ - 8), []).append(lambda p_=p_: mod_panels(p_, p_ + 1, "a"))
                side.setdefault(12 + 17 * (p_ - 8) + 13, []).append(lambda p_=p_: mod_panels(p_, p_ + 1, "b"))
            if not NOFOLD:
                for k_, t_ in enumerate(fold_tasks("w_out", 8, WOUTG, 2)):
                    side.setdefault(100 + 9 * k_, []).append(t_)
            assert max(side) < n
            for i in range(n + LOOK):
                if i < n:
                    emit_qk(i)
                if i - LOOK >= 0:
                    emit_rest(i - LOOK)
                for f_ in side.get(i, []):
                    f_()
            mod_finish()

        if STOP >= 4:
            attention()

        if DEBUG:
            for i in range(4):
                dma("sp", dv("dbg_qs", Dm["dbg_qs"][:, i, :], i, i + 1), QS[i], "dbg")
                dma("sp", dv("dbg_kn", Dm["dbg_kn"][:, i, :], i, i + 1), KPAD[i], "dbg")
                dma("sp", dv("dbg_convT", Dm["dbg_convT"][:, i, :], i, i + 1), CONVT[i], "dbg")
                dma("sp", dv("dbg_attnT", Dm["dbg_attnT"][:, i, :], i, i + 1), ATTNT[i], "dbg")
            for t in range(NT):
                dma("sp", dv("dbg_vext", Dm["dbg_vext"][:, t, :], t, t + 1),
                    V(VEXT[t].ap.rearrange("p h d -> p (h d)"), VEXT[t].regs), "dbg")
            dma("sp", dv("dbg_frows"), FROWS, "dbg")
            dma("sp", dv("dbg_negf"), NEGF, "dbg")
            dma("sp", dv("dbg_modfm"), MODFM, "dbg")

        MIX = ATTNT + CONVT
        if STOP >= 5:
            pass
        X5 = [X[0], X[1], view(O_VEXT + 16384, [128, 1024], F32), view(O_VEXT + 16384 + 4096, [128, 1024], F32)]

        def load_x5(tt_):
            s = tt_ % 4
            dma("sp", X5[s], xrows("x", tt_), "x%d" % s)

        def s5_mm(tt_):
            s = tt_ % 4
            for nh in range(2):
                P = bank()
                for kc in range(8):
                    mm(P, MIX[kc][:, tt_ * 128:(tt_ + 1) * 128], WOUTG[kc][:, nh * 512:(nh + 1) * 512], kc == 0, kc == 7)
                tt("dve", X5[s][:, nh * 512:(nh + 1) * 512], P, X5[s][:, nh * 512:(nh + 1) * 512], ALU.add)
            dma("sp", xrows("x1d", tt_), X5[s], "x1st")

        def s5_a(tt_):
            norm_tile_a(tt_, SS2, RSTD2, X5[tt_ % 4])
            if tt_ + 4 < NT:
                load_x5(tt_ + 4)

        def s5_b(tt_):
            norm_tile_b(tt_, A2, B2, lambda c, tt_=tt_: H2T[c][:, tt_ * 128:(tt_ + 1) * 128], "act")

        if STOP >= 5:
            for t_ in range(4):
                load_x5(t_)
            WDNALL = view(O_WIN, [128, NJ, 1024], BF16)
            dma("pool", WDNALL, dv("w_down", Dm["w_down"].rearrange("(j p) n -> p j n", p=128)), "wdn")
            G2B = view(O_FROWS, [128, 1024], F32)
            G2T = view(O_SCR, [128, 1024], F32)
            dma("pool", G2B, V(Dm["modd"][0:1, 5 * 1024:6 * 1024].partition_broadcast(128), [Reg("d_modd", 20, 24)]), "gate2")
            dma("pool", G2T, V(Dm["b_ada"][0:1, 5 * 1024:6 * 1024].partition_broadcast(128), [Reg("d_b_ada", 0, 1)]), "gate2")
            tt("pool", G2B, G2B, G2T, ALU.add)
            s5_mm(0)
            s5_mm(1)
            s5_a(0)
            for tt_ in range(NT):
                if tt_ + 1 < NT:
                    s5_a(tt_ + 1)
                if tt_ + 2 < NT:
                    s5_mm(tt_ + 2)
                s5_b(tt_)
        if DEBUG:
            for i in range(8):
                dma("sp", dv("dbg_h2T", Dm["dbg_h2T"][:, i, :], i, i + 1), H2T[i], "dbg")

        w_up_v = Dm["w_up"].rearrange("(kc p) n -> p kc n", p=128)

        UPF = view(O_HTC + 8192, [128, 8, 256], BF16)

        def upan(idx):
            return UPF if idx == 0 else UPAN[idx % 3]

        def issue_up(idx):
            hf, j = divmod(idx, NJ)
            s = idx % 3
            if idx == 0:
                dma("pool", UPF[:, :, 0:128], V(w_up_v[:, :, 0:128], [Reg("d_w_up", 0, 1)]), "upf")
                dma("pool", UPF[:, :, 128:256], V(w_up_v[:, :, D_FF:D_FF + 128], [Reg("d_w_up", NJ, NJ + 1)]), "upf")
                return
            dma("pool", UPAN[s][:, :, 0:128], V(w_up_v[:, :, j * 128:(j + 1) * 128], [Reg("d_w_up", j, j + 1)]), "up%d" % s)
            dma("pool", UPAN[s][:, :, 128:256], V(w_up_v[:, :, D_FF + j * 128:D_FF + (j + 1) * 128],
                                                   [Reg("d_w_up", NJ + j, NJ + j + 1)]), "up%d" % s)

        if STOP >= 6:
            for ub_ in UB:
                memset("pool", ub_[:, 0:2], 0.0)
            issue_up(0); issue_up(1)
        pair_ctr = [0]

        def ffn_head(idx):
            hf, j = divmod(idx, NJ)
            t0 = hf * 1024
            if idx + 2 < 2 * NJ:
                issue_up(idx + 2)
            U = upan(idx)
            res = []
            for gv in range(2):
                Pp = bank((pair_ctr[0] % 4) * 2, 2)
                pair_ctr[0] += 1
                for t2 in range(2):
                    for kc in range(8):
                        mm(Pp[:, t2 * 512:(t2 + 1) * 512], U[:, kc, gv * 128:(gv + 1) * 128],
                           H2T[kc][:, t0 + t2 * 512:t0 + (t2 + 1) * 512], kc == 0, kc == 7)
                ub = UB[(2 * idx + gv) % 4]
                ac = ACC[(2 * idx + gv) % 4]
                wc = (gv * NJ + j) * 3
                car = UCAR[:, (gv * NJ + j) * 2:(gv * NJ + j) * 2 + 2]
                act(ub[:, 2:1026], Pp, AF.Copy)
                act(ac, Pp, AF.Copy, scale=FCW[:, wc + 2:wc + 3])
                if hf == 0:
                    act(car, ub[:, 1024:1026], AF.Copy)
                else:
                    act(ub[:, 0:2], car, AF.Copy)
                stt(ac, ub[:, 1:1025], FCW[:, wc + 1:wc + 2], ac, ALU.mult, ALU.add)
                stt(ac, ub[:, 0:1024], FCW[:, wc:wc + 1], ac, ALU.mult, ALU.add)
                res.append(ac)
            return res

        def ffn_tail(idx, res):
            hf, j = divmod(idx, NJ)
            act(res[0], res[0], AF.Silu)
            tt("dve", GT[j], res[0], res[1], ALU.mult)

        carry = None
        for hf in range(2 if STOP >= 6 else 0):
            prev = carry
            for j in range(NJ):
                idx = hf * NJ + j
                if hf == 1 and j == 0:
                    continue
                r = ffn_head(idx)
                if prev is not None:
                    ffn_tail(*prev)
                prev = (idx, r)
            if hf == 0:
                ffn_tail(*prev)
                r = ffn_head(NJ)
                carry = (NJ, r)
            else:
                ffn_tail(*prev)
            bank_ctr[0] = rot[0].index(2 * (pair_ctr[0] % 4))
            SCRY = [view(O_PT, [128, 512], F32), view(O_SCRB, [128, 512], F32)]
            load_x(hf * 8, "x1d"); load_x(hf * 8 + 1, "x1d")
            for t8 in range(8):
                tt_ = hf * 8 + t8
                s = tt_ % 2
                for nh in range(2):
                    P = bank()
                    for j in range(NJ):
                        mm(P, GT[j][:, t8 * 128:(t8 + 1) * 128], WDN[j][:, nh * 512:(nh + 1) * 512], j == 0, j == NJ - 1)
                    yb = SCRY[nh]
                    tt("dve", yb, P, G2B[:, nh * 512:(nh + 1) * 512], ALU.mult)
                    tt("dve", X[s][:, nh * 512:(nh + 1) * 512], yb, X[s][:, nh * 512:(nh + 1) * 512], ALU.add)
                dma("sp", xrows("out", tt_), X[s], "outst")
                if t8 + 2 < 8:
                    load_x(tt_ + 2, "x1d")
        fin_regs = [Reg("d_out", 0, NT)]
        if DEBUG:
            fin_regs += [Reg("d_" + k, 0, 100000) for k in Dm if k.startswith("dbg_")]
        S.add("sp", None, R=[V(None, fin_regs)])

        S.emit(block, {"pe": s_pe, "act": s_act, "dve": s_dve, "pool": s_pool, "sp": s_sp}, dsem_handles)
    return nc


_bf = ml_dtypes.bfloat16


def _consts():
    ident = np.eye(128, dtype=np.float32)
    bones = np.zeros((128, 128), np.float32)
    bones[0:64, 0:64] = 1.0
    bones[64:128, 64:128] = 1.0
    k = np.arange(128)[:, None]
    q = np.arange(128)[None, :]
    mask = np.where(q >= k, 0.0, -30000.0).astype(np.float32)
    sel = np.zeros((128, 8, 128), np.float32)
    for h in range(8):
        for o8 in (0, 32):
            sel[o8 + h, h, :] = 1.0
    return {"k_ident": ident.astype(_bf), "k_bones": bones.astype(_bf), "k_mask": mask.astype(_bf),
            "k_sel": sel.reshape(128, 1024).astype(_bf), "k_identf": np.eye(8, dtype=np.float32),
            "k_ones": np.ones((1, 2048), np.float32).astype(_bf)}


def make_in_maps(x, c, w_ada, b_ada, norm1_g, w_in, b_forget, q_norm_g, k_norm_g,
                 conv_mix_w, w_out, norm2_g, w_up, ffn_conv_w, w_down):
    f = lambda a: np.ascontiguousarray(np.asarray(a, dtype=np.float32))
    x, c, w_ada, b_ada, norm1_g, w_in = f(x), f(c), f(w_ada), f(b_ada), f(norm1_g), f(w_in)
    b_forget, q_norm_g, k_norm_g, conv_mix_w = f(b_forget), f(q_norm_g), f(k_norm_g), f(conv_mix_w)
    w_out, norm2_g, w_up, ffn_conv_w, w_down = f(w_out), f(norm2_g), f(w_up), f(ffn_conv_w), f(w_down)
    fm = lambda v: np.ascontiguousarray(v.reshape(-1, 128).T)
    bfg = np.zeros((104, 1), np.float32)
    for o8 in (0, 32, 64, 96):
        bfg[o8:o8 + 8, 0] = b_forget[0]
    shared = {
        "w_ada": w_ada[0], "b_ada": b_ada[0].reshape(1, -1), "b_fm": fm(b_ada[0]),
        "g1n": fm(norm1_g[0]), "g2n": fm(norm2_g[0]), "w_in": w_in[0], "bfg": bfg,
        "gq": np.tile(q_norm_g[0], 2).reshape(128, 1).copy(), "gk": np.tile(k_norm_g[0], 2).reshape(128, 1).copy(),
        "cw": np.ascontiguousarray(conv_mix_w[0].reshape(3, 4, 128).transpose(2, 1, 0).reshape(128, 12)),
        "w_out": w_out[0], "w_up": w_up[0],
        "fcw": np.ascontiguousarray(ffn_conv_w[0].reshape(3, 44, 128).transpose(2, 1, 0).reshape(128, 132)),
        "w_down": w_down[0],
    }
    shared.update(_consts())
    maps = []
    for b in range(x.shape[0]):
        m = dict(shared)
        m["x"] = x[b]
        m["c"] = np.ascontiguousarray(c[b].reshape(128, 8))
        maps.append(m)
    return maps


def kernel(x, c, w_ada, b_ada, norm1_g, w_in, b_forget, q_norm_g, k_norm_g,
           conv_mix_w, w_out, norm2_g, w_up, ffn_conv_w, w_down):
    in_maps = make_in_maps(x, c, w_ada, b_ada, norm1_g, w_in, b_forget, q_norm_g, k_norm_g,
                           conv_mix_w, w_out, norm2_g, w_up, ffn_conv_w, w_down)
    nc = build_nc()
    res = run_bass_kernel_spmd(nc, in_maps, core_ids=list(range(len(in_maps))))
    out = np.stack([np.asarray(r["out"], dtype=np.float32) for r in res.results], axis=0)
    return out
```

```python
import numpy as np
import ml_dtypes
import concourse.bass as bass
import concourse.mybir as mybir
from concourse.bass_utils import run_bass_kernel_spmd

F32 = mybir.dt.float32
BF16 = mybir.dt.bfloat16
AF = mybir.ActivationFunctionType
ALU = mybir.AluOpType

S_LEN = 2048
D = 1024
NT = 16
D_IN = 3080
D_FF = 2816
NJ = 22
EPS = 1e-6
DEBUG = False
STOP = 99
NOFOLD = False
NONORM = False


class Reg:
    __slots__ = ("space", "lo", "hi")

    def __init__(self, space, lo, hi):
        self.space, self.lo, self.hi = space, lo, hi


class V:
    __slots__ = ("ap", "regs")

    def __init__(self, ap, regs):
        self.ap = ap
        self.regs = regs

    def __getitem__(self, k):
        return V(self.ap[k], self.regs)


class Op:
    __slots__ = ("eng", "fn", "deps", "sig", "seq", "dkey", "dwaits")

    def __init__(self, eng, fn):
        self.eng, self.fn = eng, fn
        self.deps = []
        self.sig = False
        self.seq = 0
        self.dkey = None
        self.dwaits = {}


ENGS = ["pe", "act", "dve", "pool", "sp"]


class Sched:
    def __init__(self):
        self.ops = {e: [] for e in ENGS}
        self.acc = {}
        self.dma_cum = {}
        self.nops = 0

    def _access(self, op, rr, ww):
        deps = {}
        for r in rr:
            tab = self.acc.setdefault(r.space, {})
            for (lo, hi, eng, isw, uid), o in tab.items():
                if isw and lo < r.hi and r.lo < hi:
                    deps[id(o)] = o
        for r in ww:
            tab = self.acc.setdefault(r.space, {})
            for (lo, hi, eng, isw, uid), o in tab.items():
                if lo < r.hi and r.lo < hi:
                    deps[id(o)] = o
        for r in ww:
            tab = self.acc[r.space]
            for k in [k for k in tab if r.lo <= k[0] and k[1] <= r.hi]:
                del tab[k]
        self.nops += 1
        uid = self.nops if op.dkey is not None else 0
        for r in ww:
            self.acc[r.space][(r.lo, r.hi, op.eng, True, uid)] = op
        for r in rr:
            self.acc[r.space][(r.lo, r.hi, op.eng, False, uid)] = op
        deps.pop(id(op), None)
        return list(deps.values())

    def add(self, eng, fn, R=(), W=(), dma=None):
        op = Op(eng, fn)
        op.dkey = dma
        rr = [g for v in R for g in v.regs]
        ww = [g for v in W for g in v.regs]
        op.deps = self._access(op, rr, ww)
        for d in op.deps:
            if d.dkey is not None:
                op.dwaits[d.dkey] = self.dma_cum[d.dkey]
        if dma is not None:
            self.dma_cum[dma] = self.dma_cum.get(dma, 0) + 16
        self.ops[eng].append(op)
        return op

    def emit(self, block, sems, dsems):
        for e in ENGS:
            for op in self.ops[e]:
                for d in op.deps:
                    if d.dkey is not None or (d.eng == "pe" and op.eng == "pe"):
                        continue
                    d.sig = True
        for e in ENGS:
            n = 0
            for op in self.ops[e]:
                if op.sig:
                    n += 1
                    op.seq = n

        def body(e):
            def run(eng):
                waited = {}
                for op in self.ops[e]:
                    need = {}
                    for d in op.deps:
                        if d.dkey is not None or (d.eng == "pe" and op.eng == "pe"):
                            continue
                        need[d.eng] = max(need.get(d.eng, 0), d.seq)
                    for pe_, v in need.items():
                        if waited.get(("e", pe_), 0) < v:
                            eng.wait_ge(sems[pe_], v)
                            waited[("e", pe_)] = v
                    for k, v in op.dwaits.items():
                        if waited.get(("d", k), 0) < v:
                            eng.wait_ge(dsems[k], v)
                            waited[("d", k)] = v
                    if op.fn is None:
                        continue
                    ins = op.fn(eng)
                    if op.dkey is not None:
                        ins.then_inc(dsems[op.dkey], 16)
                    elif op.sig:
                        ins.then_inc(sems[e], 1)
            return run

        block.tensor(body("pe"))
        block.scalar(body("act"))
        block.vector(body("dve"))
        block.gpsimd(body("pool"))
        block.sync(body("sp"))


DMA_KEYS = ["small", "x0", "x1", "wada0", "wada1", "win0", "win1", "win2", "win3", "win4", "wfg", "modd", "gate", "wst0", "wst1",
            "up0", "up1", "up2", "x1st", "outst", "dbg", "x2", "x3", "wada2", "wada3", "small2", "wdn", "gate2", "wadaL0", "wadaL1", "upf"]

ARENA_BYTES = 207872


def build_nc():
    nc = bass.Bass("TRN2", target_bir_lowering=False)
    Dm = {}

    def din(name, shape, dt=F32):
        Dm[name] = nc.dram_tensor(name, shape, dt, kind="ExternalInput").ap()

    din("x", [S_LEN, D]); din("c", [128, 8]); din("w_ada", [D, 6 * D]); din("b_fm", [128, 48])
    din("b_ada", [1, 6 * D]); din("g1n", [128, 8]); din("g2n", [128, 8])
    din("w_in", [D, D_IN]); din("bfg", [104, 1]); din("gq", [128, 1]); din("gk", [128, 1])
    din("cw", [128, 12]); din("w_out", [D, D]); din("w_up", [D, 2 * D_FF]); din("fcw", [128, 132])
    din("w_down", [D_FF, D])
    din("k_ident", [128, 128], BF16); din("k_bones", [128, 128], BF16); din("k_mask", [128, 128], BF16)
    din("k_sel", [128, 1024], BF16); din("k_identf", [8, 8]); din("k_ones", [1, 2048], BF16)
    Dm["out"] = nc.dram_tensor("out", [S_LEN, D], F32, kind="ExternalOutput").ap()
    Dm["x1d"] = nc.dram_tensor("x1d", [S_LEN, D], F32).ap()
    Dm["modd"] = nc.dram_tensor("modd", [1, 6 * D], F32).ap()
    dbg_specs = {}
    if DEBUG:
        for nm, shp, dt in [("dbg_hT", [128, 8, 512], BF16), ("dbg_qs", [128, 4, 2048], BF16),
                            ("dbg_kn", [128, 4, 2048], BF16), ("dbg_vext", [128, 16, 1024], BF16),
                            ("dbg_frows", [64, 2048], BF16), ("dbg_negf", [128, 128], F32),
                            ("dbg_convT", [128, 4, 2048], BF16), ("dbg_attnT", [128, 4, 2048], BF16),
                            ("dbg_h2T", [128, 8, 2048], BF16), ("dbg_modfm", [128, 48], F32)]:
            Dm[nm] = nc.dram_tensor(nm, shp, dt, kind="ExternalOutput").ap()

    def dv(name, ap=None, lo=0, hi=1):
        return V(Dm[name] if ap is None else ap, [Reg("d_" + name, lo, hi)])

    with (
        nc.sbuf_tensor("arena", [128, ARENA_BYTES // 4], F32) as at,
        nc.psum_tensor("ps", [128, 4096], F32) as pst,
        nc.semaphore("s_pe") as s_pe, nc.semaphore("s_act") as s_act, nc.semaphore("s_dve") as s_dve,
        nc.semaphore("s_pool") as s_pool, nc.semaphore("s_sp") as s_sp,
        nc.Block() as block,
    ):
        dsem_handles = {k: nc.alloc_semaphore(name="dq_" + k) for k in DMA_KEYS}
        S = Sched()

        def view(off, shape, dt):
            esz = 2 if dt == BF16 else 4
            nb = int(np.prod(shape[1:])) * esz
            assert off % 4 == 0 and nb % 4 == 0 and off + nb <= ARENA_BYTES, (off, nb)
            ap = at[0:shape[0], off // 4:(off + nb) // 4]
            if dt != F32:
                ap = ap.bitcast(dt)
            if len(shape) == 3:
                ap = ap.rearrange("p (a b) -> p a b", b=shape[2])
            return V(ap, [Reg("sb", off, off + nb)])

        bank_ctr = [0]
        rot = [[0, 1, 2, 3, 4, 5, 6]]

        def bank(b=None, n=1):
            if b is None:
                b = rot[0][bank_ctr[0] % len(rot[0])]
                bank_ctr[0] += 1
            return V(pst[:, b * 512:(b + n) * 512], [Reg("ps", b * 2048, (b + n) * 2048)])

        def bankbf(bk):
            return V(bk.ap.bitcast(BF16), bk.regs)

        O_WIN, O_HTC, O_QK, O_VEXT, O_CONVT, O_ATTNT = 0, 49280, 65664, 98432, 131200, 147584
        O_FROWS, O_PT, O_NEGF, O_X, O_XN, O_SQJ, O_SCR, O_SCRB, O_SMALL = (
            163968, 168064, 171136, 171648, 179840, 183936, 185984, 198320, 200368)
        WIN = [view(O_WIN + kc * 6160, [128, D_IN], BF16) for kc in range(8)]
        WDN = [view(O_WIN + j * 2048, [128, 1024], BF16) for j in range(NJ)]
        HTC = [view(O_HTC + s * 8192, [128, 8, 512], BF16) for s in range(2)]
        WOUTG = [view(O_HTC + kc * 2048, [128, 1024], BF16) for kc in range(8)]
        ACC = [view(O_HTC + s * 4096, [128, 1024], F32) for s in range(4)]
        QS = [view(O_QK + c * 4096, [128, 2048], BF16) for c in range(4)]
        KPAD = [view(O_QK + 16384 + h * 4096, [128, 2048], BF16) for h in range(8)]
        WINALL = view(O_WIN, [128, 8, D_IN], BF16)
        H2T = [view(O_QK + kc * 4096, [128, 2048], BF16) for kc in range(8)]
        VEXT = [view(O_WIN + t * 2048, [128, 8, 128], BF16) for t in range(NT)]
        VTOK = [view(O_VEXT + 16384 + t * 1024, [128, 8, 64], BF16) for t in range(NT)]
        QPAD = [view(O_VEXT + 16384 + i * 4096, [128, 2048], BF16) for i in range(4)]
        GT = [view(O_VEXT + j * 2048, [128, 1024], BF16) for j in range(NJ)]
        UB = [view(O_VEXT + 45056 + s * 4104, [128, 1026], F32) for s in range(4)]
        CONVT = [view(O_CONVT + j * 4096, [128, 2048], BF16) for j in range(4)]
        ATTNT = [view(O_ATTNT + c * 4096, [128, 2048], BF16) for c in range(4)]
        WADA = [view(O_ATTNT + s * 4096, [128, 8, 256], BF16) for s in range(4)]
        FSCR = [view(O_ATTNT + 8192 + s * 2048, [104, 512], F32) for s in range(3)]
        FHI2 = view(O_ATTNT + 8192 + 6144, [104, 512], BF16)
        FROWS = view(O_FROWS, [128, 2048], BF16)
        PT = [view(O_PT + s * 1024, [128, 512], BF16) for s in range(3)]
        ONESF = view(O_PT, [104, 512], F32)
        NEGF = view(O_NEGF, [128, 128], F32)
        X = [view(O_X + s * 4096, [128, 1024], F32) for s in range(2)]
        XN = [view(O_XN + s * 2048, [128, 1024], BF16) for s in range(2)]
        SQJ = view(O_SQJ, [128, 1024], BF16)
        UPAN = [view(O_XN + s * 4096, [128, 8, 256], BF16) for s in range(3)]
        SCR = [view(O_SCR + s * 2056, [128, 514], F32) for s in range(6)]
        GATEB = [view(O_SCR + s * 4096, [128, 1024], F32) for s in range(2)]
        SCRB = [view(O_SCRB + s * 1024, [128, 512], BF16) for s in range(2)]
        ROWB = [view(O_SCRB + s * 1024, [1, 256], F32) for s in range(2)]
        so = [O_SMALL]

        def small(shape, dt=F32):
            esz = 2 if dt == BF16 else 4
            nb = (int(np.prod(shape[1:])) * esz + 31) // 32 * 32
            v = view(so[0], shape, dt)
            so[0] += nb
            return v

        IDENT = small([128, 128], BF16); BONES = small([128, 128], BF16); MASK = small([128, 128], BF16)
        SEL = small([128, 1024], BF16); IDENTF = small([8, 8]); CVEC = small([128, 8]); CACT = small([128, 8], BF16)
        G1N = small([128, 8]); G2N = small([128, 8]); BFM = small([128, 48]); MODFM = small([128, 48])
        A1 = small([128, 8]); A2 = small([128, 8]); B1 = small([128, 8]); B2 = small([128, 8])
        CW = small([128, 12]); FCW = small([128, 132]); GQ = small([128, 1]); GK = small([128, 1]); GQK = small([128, 1])
        BFG = small([104, 1]); NBFG = small([104, 1]); EPSV = small([128, 1]); ONE1 = small([128, 1])
        SSQ = small([128, 16]); RSTD = small([128, 16]); SS2 = small([128, 16]); RSTD2 = small([128, 16])
        FCAR = small([104, 1]); ZCAR = small([128, 8]); UCAR = small([128, 88]); WFG = small([128, 8, 104], BF16)
        assert so[0] <= ARENA_BYTES, so[0]

        def sr(x):
            return ([x] if isinstance(x, V) else []), (x.ap if isinstance(x, V) else x)

        def mm(out, lhsT, rhs, st, sp):
            S.add("pe", lambda e: e.matmul(out.ap, lhsT.ap, rhs.ap, start=st, stop=sp), R=[lhsT, rhs], W=[out])

        def tr(out, in_, ident):
            S.add("pe", lambda e: e.transpose(out.ap, in_.ap, ident.ap), R=[in_, ident], W=[out])

        def act(out, in_, func, bias=None, scale=1.0, accum=None):
            rb, bv = sr(bias) if bias is not None else ([], None)
            rs, sv = sr(scale)
            kw = {}
            if bv is not None:
                kw["bias"] = bv
            if accum is not None:
                kw["accum_out"] = accum.ap
            S.add("act", lambda e: e.activation(out=out.ap, in_=in_.ap, func=func, scale=sv, **kw),
                  R=[in_] + rb + rs, W=[out] + ([accum] if accum is not None else []))

        def tt(eng, out, a, b, op):
            S.add(eng, lambda e: e.tensor_tensor(out=out.ap, in0=a.ap, in1=b.ap, op=op), R=[a, b], W=[out])

        def ts(eng, out, in0, s1, op0, s2=None, op1=None):
            r1, v1 = sr(s1)
            r2, v2 = sr(s2) if s2 is not None else ([], None)
            if op1 is None:
                S.add(eng, lambda e: e.tensor_scalar(out=out.ap, in0=in0.ap, scalar1=v1, scalar2=None, op0=op0),
                      R=[in0] + r1, W=[out])
            else:
                S.add(eng, lambda e: e.tensor_scalar(out=out.ap, in0=in0.ap, scalar1=v1, scalar2=v2, op0=op0, op1=op1),
                      R=[in0] + r1 + r2, W=[out])

        def stt(out, in0, scalar, in1, op0, op1):
            r1, v1 = sr(scalar)
            S.add("dve", lambda e: e.scalar_tensor_tensor(out=out.ap, in0=in0.ap, scalar=v1, in1=in1.ap, op0=op0, op1=op1),
                  R=[in0, in1] + r1, W=[out])

        def cp(eng, out, in_):
            S.add(eng, lambda e: e.tensor_copy(out=out.ap, in_=in_.ap), R=[in_], W=[out])

        def recip(out, in_):
            S.add("dve", lambda e: e.reciprocal(out=out.ap, in_=in_.ap), R=[in_], W=[out])

        def memset(eng, out, val):
            S.add(eng, lambda e: e.memset(out.ap, val), W=[out])

        def dma(q, out, in_, key):
            S.add(q, lambda e: e.dma_start(out=out.ap, in_=in_.ap), R=[in_], W=[out], dma=key)

        def xrows(name, tt_):
            return dv(name, Dm[name][tt_ * 128:(tt_ + 1) * 128, :], tt_, tt_ + 1)

        for dst, nm in [(IDENT, "k_ident"), (BONES, "k_bones"), (MASK, "k_mask"), (SEL, "k_sel"), (IDENTF, "k_identf"),
                        (CVEC, "c"), (G1N, "g1n"), (G2N, "g2n"), (BFM, "b_fm"), (CW, "cw"), (FCW, "fcw"),
                        (GQ, "gq"), (GK, "gk"), (BFG, "bfg")]:
            dma("sp", dst, dv(nm), "small")
        memset("dve", EPSV, EPS)
        memset("dve", ONE1, 1.0)
        memset("dve", SSQ, 0.0)
        memset("dve", SS2, 0.0)
        memset("dve", ZCAR, 0.0)
        memset("dve", FROWS, 0.0)
        memset("dve", ONESF, 1.0)
        memset("pool", WFG, 0.0)
        def kpad_init():
            for h in range(8):
                zr = 64 if h % 2 == 0 else 0
                memset("dve", KPAD[h][zr:zr + 64, :], 0.0)
            for h in range(8):
                zr = 64 if h % 2 == 0 else 0
                for o8 in (0, 32):
                    dma("sp", KPAD[h][zr + o8 + h:zr + o8 + h + 1, :], dv("k_ones"), "small2")
        ts("dve", NBFG, BFG, -1.0, ALU.mult)
        stt(GQK, GQ, 0.125, GK, ALU.mult, ALU.mult)
        act(CACT, CVEC, AF.Silu)
        xslot = {}

        def load_x(tt_, src="x"):
            s = tt_ % 2
            dma("sp", X[s], xrows(src, tt_), "x%d" % s)

        load_x(0); load_x(1)

        w_ada_v = Dm["w_ada"].rearrange("(p j) n -> p j n", j=8)
        panel_order = [4, 5, 6, 7, 0, 1, 2, 3] + list(range(8, 24))
        mod_ps = bank(7)
        n_issued = [0]

        WADAL = [view(O_X + s * 4096, [128, 8, 256], BF16) for s in range(2)]
        ROWBL = [view(O_SQJ + s * 1024, [1, 256], F32) for s in range(2)]
        late = {"on": False}

        def pslot(i):
            return i % 4 if i < 8 else i % 2

        def issue_panel(i):
            pn = panel_order[i]
            s = pslot(i)
            if i >= 8:
                dma("pool", WADAL[s], V(w_ada_v[:, :, pn * 256:(pn + 1) * 256], [Reg("d_w_ada", pn, pn + 1)]), "wadaL%d" % s)
                return
            dma("pool", WADA[s], V(w_ada_v[:, :, pn * 256:(pn + 1) * 256], [Reg("d_w_ada", pn, pn + 1)]), "wada%d" % s)

        w_in_v = Dm["w_in"].rearrange("(kc p) n -> p kc n", p=128)
        WIN_PANELS = [(0, 512), (512, 1024), (1024, 1544), (1544, 2312), (2312, 3080)]

        def wregs(c0, c1, kcs=range(8)):
            return [Reg("sb", O_WIN + kc * 6160 + c0 * 2, O_WIN + kc * 6160 + c1 * 2) for kc in kcs]

        def win_load(pi):
            if pi < len(WIN_PANELS):
                c0, c1 = WIN_PANELS[pi]
                dma("pool", V(WINALL.ap[:, :, c0:c1], wregs(c0, c1)), V(w_in_v[:, :, c0:c1], [Reg("d_w_in", pi, pi + 1)]), "win%d" % pi)

        def wv(kc, c0, c1):
            return V(WIN[kc].ap[:, c0:c1], wregs(c0, c1, [kc]))

        for i_ in range(4):
            issue_panel(i_)
        win_load(0)
        kpad_init()

        def mod_panels(lo, hi, part="ab"):
          for i in range(lo, hi):
            pn = panel_order[i]
            if part == "b":
                mod_part_b(i)
                continue
            s = pslot(i)
            rp = (V(pst[:, 7 * 512 + 256:8 * 512], [Reg("ps", 7 * 2048, 8 * 2048)]) if i >= 8 else bank())
            wsl = WADAL[s] if i >= 8 else WADA[s]
            ROWB_ = ROWBL if i >= 8 else ROWB
            for j in range(8):
                mm(rp[0:1, 0:256], CACT[:, j:j + 1], wsl[:, j, :], j == 0, j == 7)
            if i >= 8:
                cp("dve", ROWB_[i % 2], rp[0:1, 0:256])
            else:
                act(ROWB_[i % 2], rp[0:1, 0:256], AF.Copy)
            if i < 4:
                issue_panel(i + 4)
            elif 8 <= i and i + 2 < 24:
                issue_panel(i + 2)
            if i == 3:
                for pi_ in range(1, 5):
                    win_load(pi_)
            if i == 7:
                for o8 in (0, 32, 64, 96):
                    cp("pool", WFG[:, :, o8:o8 + 8], V(WINALL.ap[:, :, 1536:1544], wregs(1536, 1544)))
            if part == "a":
                continue
            mod_part_b(i)

        def mod_part_b(i):
            pn = panel_order[i]
            ROWB_ = ROWBL if i >= 8 else ROWB
            sect = pn // 4
            if sect in (2, 5):
                dma("sp", dv("modd", Dm["modd"][0:1, pn * 256:(pn + 1) * 256], pn, pn + 1), ROWB_[i % 2], "modd")
            else:
                for hh in range(2):
                    col = pn * 2 + hh
                    mm(mod_ps[:, col:col + 1], ROWB_[i % 2][0:1, hh * 128:(hh + 1) * 128], ONE1[0:1, 0:1], True, True)
            if i == 7:
                tt("dve", MODFM[:, 0:16], mod_ps[:, 0:16], BFM[:, 0:16], ALU.add)
                stt(A1, MODFM[:, 8:16], 1.0, G1N, ALU.add, ALU.mult)
                cp("dve", B1, MODFM[:, 0:8])
        def mod_finish():
            tt("dve", MODFM[:, 24:40], mod_ps[:, 24:40], BFM[:, 24:40], ALU.add)
            stt(A2, MODFM[:, 32:40], 1.0, G2N, ALU.add, ALU.mult)
            cp("dve", B2, MODFM[:, 24:32])
            rot[0] = [0, 1, 2, 3, 4, 5, 6, 7]

        mod_panels(0, 8)
        if STOP <= 1:
            mod_panels(8, 24)
            mod_finish()

        def norm_tile_a(tt_, SSv, RSv, xsrc=None):
            s = tt_ % 2
            xsrc = X[s] if xsrc is None else xsrc
            act(SQJ, xsrc, AF.Square, accum=SSv[:, tt_:tt_ + 1])
            act(RSv[:, tt_:tt_ + 1], SSv[:, tt_:tt_ + 1], AF.Ln, bias=EPSV, scale=1.0 / D)
            act(RSv[:, tt_:tt_ + 1], RSv[:, tt_:tt_ + 1], AF.Exp, scale=-0.5)
            ts("dve", XN[s], xsrc, RSv[:, tt_:tt_ + 1], ALU.mult)

        def norm_tile_b(tt_, Av, Bv, dst_fn, evac_eng="dve"):
            s = tt_ % 2
            pb = bank()
            pbb = bankbf(pb)
            if evac_eng == "act":
                pbb2 = bankbf(bank())
                for c in range(8):
                    src = (pbb if c % 2 == 0 else pbb2)[:, (c // 2) * 128:(c // 2 + 1) * 128]
                    tr(src, XN[s][:, c * 128:(c + 1) * 128], IDENT)
                for c in range(8):
                    src = (pbb if c % 2 == 0 else pbb2)[:, (c // 2) * 128:(c // 2 + 1) * 128]
                    if c % 2 == 0:
                        act(dst_fn(c), src, AF.Identity, bias=Bv[:, c:c + 1], scale=Av[:, c:c + 1])
                    else:
                        ts("dve", dst_fn(c), src, Av[:, c:c + 1], ALU.mult, Bv[:, c:c + 1], ALU.add)
                return
            for c in range(8):
                tr(pbb[:, c * 128:(c + 1) * 128], XN[s][:, c * 128:(c + 1) * 128], IDENT)
            for c in range(8):
                ts("dve", dst_fn(c), pbb[:, c * 128:(c + 1) * 128], Av[:, c:c + 1], ALU.mult, Bv[:, c:c + 1], ALU.add)

        def norm_tile(tt_, SSv, RSv, Av, Bv, dst_fn, xsrc=None, evac_eng="dve"):
            norm_tile_a(tt_, SSv, RSv, xsrc)
            norm_tile_b(tt_, Av, Bv, dst_fn, evac_eng)

        def s2_a(tt_):
            norm_tile_a(tt_, SSQ, RSTD)
            if tt_ + 2 < NT:
                load_x(tt_ + 2)

        def pipe_step(tt_):
            tc, t4 = divmod(tt_, 4)
            norm_tile_b(tt_, A1, B1, lambda c, t4=t4, tc=tc: HTC[tc % 2][:, c, t4 * 128:(t4 + 1) * 128])
            if tt_ + 1 < NT:
                s2_a(tt_ + 1)

        scr_ctr = [0]

        def scr():
            v = SCR[scr_ctr[0] % 6]
            scr_ctr[0] += 1
            return v

        def stage3(tc, between=None):
            between = between or (lambda k: None)
            H = HTC[tc % 2]
            cols = slice(tc * 512, (tc + 1) * 512)
            def qk_finish(nq, sqb, qc):
                P2 = bank()
                mm(P2, BONES, sqb, True, True)
                rs = scr()
                act(rs[:, 0:512], P2, AF.Ln, bias=EPSV, scale=1.0 / 64)
                act(rs[:, 0:512], rs[:, 0:512], AF.Exp, scale=-0.5)
                if nq < 4:
                    stt(QS[nq][:, cols], qc[:, 0:512], GQK, rs[:, 0:512], ALU.mult, ALU.mult)
                else:
                    hA = 2 * (nq - 4)
                    tt("dve", KPAD[hA][0:64, cols], qc[0:64, 0:512], rs[0:64, 0:512], ALU.mult)
                    tt("dve", KPAD[hA + 1][64:128, cols], qc[64:128, 0:512], rs[64:128, 0:512], ALU.mult)

            pend = None
            for nq in range(8):
                P = bank()
                for kc in range(8):
                    mm(P, wv(kc, nq * 128, (nq + 1) * 128), H[:, kc, :], kc == 0, kc == 7)
                sqb = SCRB[nq % 2]
                act(sqb, P, AF.Square)
                qc = scr()
                act(qc[:, 0:512], P, AF.Copy)
                if pend is not None:
                    qk_finish(*pend)
                pend = (nq, sqb, qc)
                if nq == 3:
                    between(0)
                if nq == 7:
                    between(1)
            for t4 in range(4):
                tt_ = tc * 4 + t4
                P = bank()
                for kc in range(8):
                    mm(P, H[:, kc, t4 * 128:(t4 + 1) * 128], wv(kc, 1024, 1536), kc == 0, kc == 7)
                S.add("act", lambda e, P=P, tt_=tt_: e.activation(
                    out=VTOK[tt_].ap, in_=P.ap.rearrange("p (h d) -> p h d", d=64), func=AF.Copy),
                    R=[P], W=[VTOK[tt_]])
                if pend is not None:
                    qk_finish(*pend)
                    pend = None
            between(2)
            P = bank()
            for kc in range(8):
                mm(P[0:104, :], WFG[:, kc, :], H[:, kc, :], kc == 0, kc == 7)
            e_ = FSCR[0]; l_ = FSCR[1]; f_ = FSCR[2]
            act(e_, P[0:104, :], AF.Exp, bias=NBFG, scale=-1.0)
            act(l_, e_, AF.Ln, bias=ONE1[0:104, :], scale=1.0)
            init = 0.0 if tc == 0 else FCAR
            ri, iv = sr(init)
            S.add("dve", lambda e, iv=iv: e.tensor_tensor_scan(out=f_.ap, data0=ONESF.ap, data1=l_.ap, initial=iv,
                                                             op0=ALU.mult, op1=ALU.subtract),
                  R=[ONESF, l_] + ri, W=[f_])
            cp("dve", FCAR, f_[:, 511:512])
            for b0 in (0, 64):
                cp("dve", FROWS[b0:b0 + 8, cols], f_[b0:b0 + 8, :])
                cp("dve", FHI2[b0 + 32:b0 + 40, :], f_[b0 + 32:b0 + 40, :])
                tt("dve", FROWS[b0 + 32:b0 + 40, cols], f_[b0 + 32:b0 + 40, :], FHI2[b0 + 32:b0 + 40, :], ALU.subtract)
            for j in range(4):
                Px = bank(); Pb = bank(); Pc = bank()
                for (Pq, base) in ((Px, 1544), (Pb, 2056), (Pc, 2568)):
                    for kc in range(8):
                        mm(Pq, wv(kc, base + j * 128, base + (j + 1) * 128), H[:, kc, :], kc == 0, kc == 7)
                xs = scr()
                act(xs[:, 0:512], Px, AF.Copy)
                z = scr()
                tt("dve", z[:, 2:514], Pc, xs[:, 0:512], ALU.mult)
                cp("dve", z[:, 0:2], ZCAR[:, 2 * j:2 * j + 2])
                cp("dve", ZCAR[:, 2 * j:2 * j + 2], z[:, 512:514])
                a_ = scr()
                ts("dve", a_[:, 0:512], z[:, 0:512], CW[:, j * 3:j * 3 + 1], ALU.mult)
                stt(a_[:, 0:512], z[:, 1:513], CW[:, j * 3 + 1:j * 3 + 2], a_[:, 0:512], ALU.mult, ALU.add)
                stt(a_[:, 0:512], z[:, 2:514], CW[:, j * 3 + 2:j * 3 + 3], a_[:, 0:512], ALU.mult, ALU.add)
                tt("dve", CONVT[j][:, cols], Pb, a_[:, 0:512], ALU.mult)
                if j == 1:
                    between(3)
            Pt = bank()
            for t4 in range(4):
                tr(Pt[:, t4 * 8:(t4 + 1) * 8], f_[0:8, t4 * 128:(t4 + 1) * 128], IDENTF)
            ts("dve", NEGF[:, tc * 32:(tc + 1) * 32], Pt[:, 0:32], -1.0, ALU.mult)

        if STOP >= 2:
          s2_a(0)
          for t_ in range(4):
              pipe_step(t_)
          for tc in range(4):
            if STOP >= 3:
                stage3(tc, (lambda k, tc=tc: pipe_step(4 * (tc + 1) + k)) if tc + 1 < 4 else None)
            elif tc + 1 < 4:
                for k in range(4):
                    pipe_step(4 * (tc + 1) + k)
            pass
        for _ in range(0):
            pass
            if DEBUG and tc == 0:
                dma("sp", dv("dbg_hT"), HTC[0], "dbg")

        WSTG = [view(O_SCR + 4 * 2056, [128, 1024], F32), view(O_FROWS, [128, 1024], F32)]

        XNSTG = view(O_XN, [128, 1024], F32)

        def fold_tasks(wname, nchunks, dst, gate_sect):
            tasks = []

            def t_gate():
                dma("sp", GATEB[0], V(Dm["modd"][0:1, gate_sect * 1024:(gate_sect + 1) * 1024].partition_broadcast(128),
                                      [Reg("d_modd", gate_sect * 4, gate_sect * 4 + 4)]), "gate")
                dma("sp", GATEB[1], V(Dm["b_ada"][0:1, gate_sect * 1024:(gate_sect + 1) * 1024].partition_broadcast(128),
                                      [Reg("d_b_ada", 0, 1)]), "gate")
            tasks.append(t_gate)

            def t_first():
                tt("dve", GATEB[0], GATEB[0], GATEB[1], ALU.add)
                dma("sp", XNSTG, dv(wname, Dm[wname][0:128, :], 0, 1), "wst0")
            tasks.append(t_first)
            for kc in range(nchunks):
                def t_chunk(kc=kc):
                    tt("dve", dst[kc], XNSTG, GATEB[0], ALU.mult)
                    if kc + 1 < nchunks:
                        dma("sp", XNSTG, dv(wname, Dm[wname][(kc + 1) * 128:(kc + 2) * 128, :], kc + 1, kc + 2), "wst0")
                tasks.append(t_chunk)
            return tasks

        def attention():
            LOOK = 3
            steps = [(h, qc, kb) for h in range(8) for qc in range(4) for kb in range(4 * qc + 4)]
            sbank = {}
            state = {"A": None, "a_ctr": 0}

            def geom(h, qc, kb):
                d_ = kb - 4 * qc
                col0 = max(0, d_) * 128
                return d_, col0, qc * 512 + col0

            def emit_qk(i):
                h, qc, kb = steps[i]
                c = h // 2
                rows = slice((h % 2) * 64, (h % 2) * 64 + 64)
                d_, col0, q0 = geom(h, qc, kb)
                Sb = bank(3 + i % 4)
                sbank[i] = Sb
                if kb == 0 and (qc == 0 or (h == 0 and qc == 1)):
                    for hb in (([0] if qc == 0 else [1]) if h == 0 else ([h + 1] if h + 1 < 8 else [])):
                        cb = hb // 2
                        dr_ = slice((hb % 2) * 64, (hb % 2) * 64 + 64)
                        fr_ = slice(64 - (hb % 2) * 64, 128 - (hb % 2) * 64)
                        cp("dve", QPAD[hb % 4][dr_, :], QS[cb][dr_, :])
                        cp("dve", QPAD[hb % 4][fr_, :], FROWS[fr_, :])
                mm(Sb[:, col0:512], KPAD[h][:, kb * 128:(kb + 1) * 128], QPAD[h % 4][:, q0:(qc + 1) * 512], True, d_ < 0)
                if d_ >= 0:
                    mm(Sb[:, col0:col0 + 128], IDENT, MASK, False, True)

            def emit_rest(i):
                h, qc, kb = steps[i]
                c = h // 2
                r0 = (h % 2) * 64
                nkb = 4 * qc + 4
                d_, col0, q0 = geom(h, qc, kb)
                if kb == 0:
                    state["A"] = bank(state["a_ctr"] % 3)
                    state["a_ctr"] += 1
                A = state["A"]
                Sb = sbank.pop(i)
                P_ = PT[i % 3]
                act(P_[:, col0:512], Sb[:, col0:512], AF.Exp, bias=NEGF[:, kb * 8 + h:kb * 8 + h + 1], scale=1.0)
                mm(A[:, col0:512], VEXT[kb][:, h, :], P_[:, col0:512], kb == 0, kb == nkb - 1)
                if kb == nkb - 1:
                    rd = SCR[4 + state["a_ctr"] % 2]
                    recip(rd[0:64, 0:512], A[64:128, :])
                    ocols = slice(qc * 512, (qc + 1) * 512)
                    tt("dve", ATTNT[c][r0:r0 + 64, ocols], A[0:64, :], rd[0:64, 0:512], ALU.mult)

            VEXTA = view(O_WIN, [128, 32, 128], BF16)
            VTOKA = view(O_VEXT + 16384, [128, 32, 64], BF16)
            VEXTB = view(O_WIN + 8192, [128, 96, 128], BF16)
            VTOKB = view(O_VEXT + 16384 + 4096, [128, 96, 64], BF16)
            memset("pool", VEXTA[:, :, 64:128], 1.0)
            cp("dve", VEXTA[:, :, 0:64], VTOKA)

            def expand_rest():
                memset("pool", VEXTB[:, :, 64:128], 1.0)
                cp("dve", VEXTB[:, :, 0:64], VTOKB)
            n = len(steps)
            issue_panel(8); issue_panel(9)
            side = {}
            for p_ in range(8, 24):
                side.setdefault(12 + 17 * (p_ - 8), []).append(lambda p_=p_: mod_panels(p_, p_ + 1, "a"))
                side.setdefault(12 + 17 * (p_ - 8) + 13, []).append(lambda p_=p_: mod_panels(p_, p_ + 1, "b"))
            if not NOFOLD:
                for k_, t_ in enumerate(fold_tasks("w_out", 8, WOUTG, 2)):
                    side.setdefault(100 + 9 * k_, []).append(t_)
            assert max(side) < n
            for i in range(n + LOOK):
                if i < n:
                    emit_qk(i)
                if i == 0:
                    expand_rest()
                if i - LOOK >= 0:
                    emit_rest(i - LOOK)
                for f_ in side.get(i, []):
                    f_()
            mod_finish()

        if STOP >= 4:
            attention()

        if DEBUG:
            for i in range(4):
                dma("sp", dv("dbg_qs", Dm["dbg_qs"][:, i, :], i, i + 1), QS[i], "dbg")
                dma("sp", dv("dbg_kn", Dm["dbg_kn"][:, i, :], i, i + 1), KPAD[i], "dbg")
                dma("sp", dv("dbg_convT", Dm["dbg_convT"][:, i, :], i, i + 1), CONVT[i], "dbg")
                dma("sp", dv("dbg_attnT", Dm["dbg_attnT"][:, i, :], i, i + 1), ATTNT[i], "dbg")
            for t in range(NT):
                dma("sp", dv("dbg_vext", Dm["dbg_vext"][:, t, :], t, t + 1),
                    V(VEXT[t].ap.rearrange("p h d -> p (h d)"), VEXT[t].regs), "dbg")
            dma("sp", dv("dbg_frows"), FROWS, "dbg")
            dma("sp", dv("dbg_negf"), NEGF, "dbg")
            dma("sp", dv("dbg_modfm"), MODFM, "dbg")

        MIX = ATTNT + CONVT
        if STOP >= 5:
            pass
        X5 = [X[0], X[1], view(O_VEXT + 16384, [128, 1024], F32), view(O_VEXT + 16384 + 4096, [128, 1024], F32)]

        def load_x5(tt_):
            s = tt_ % 4
            dma("sp", X5[s], xrows("x", tt_), "x%d" % s)

        def s5_mm(tt_):
            s = tt_ % 4
            for nh in range(2):
                P = bank()
                for kc in range(8):
                    mm(P, MIX[kc][:, tt_ * 128:(tt_ + 1) * 128], WOUTG[kc][:, nh * 512:(nh + 1) * 512], kc == 0, kc == 7)
                tt("dve", X5[s][:, nh * 512:(nh + 1) * 512], P, X5[s][:, nh * 512:(nh + 1) * 512], ALU.add)
            dma("sp", xrows("x1d", tt_), X5[s], "x1st")

        def s5_a(tt_):
            norm_tile_a(tt_, SS2, RSTD2, X5[tt_ % 4])
            if tt_ + 4 < NT:
                load_x5(tt_ + 4)

        def s5_b(tt_):
            norm_tile_b(tt_, A2, B2, lambda c, tt_=tt_: H2T[c][:, tt_ * 128:(tt_ + 1) * 128], "act")

        if STOP >= 5:
            for t_ in range(4):
                load_x5(t_)
            WDNALL = view(O_WIN, [128, NJ, 1024], BF16)
            dma("pool", WDNALL, dv("w_down", Dm["w_down"].rearrange("(j p) n -> p j n", p=128)), "wdn")
            G2B = view(O_FROWS, [128, 1024], F32)
            G2T = view(O_SCR, [128, 1024], F32)
            dma("pool", G2B, V(Dm["modd"][0:1, 5 * 1024:6 * 1024].partition_broadcast(128), [Reg("d_modd", 20, 24)]), "gate2")
            dma("pool", G2T, V(Dm["b_ada"][0:1, 5 * 1024:6 * 1024].partition_broadcast(128), [Reg("d_b_ada", 0, 1)]), "gate2")
            tt("pool", G2B, G2B, G2T, ALU.add)
            s5_mm(0)
            s5_mm(1)
            s5_a(0)
            for tt_ in range(NT):
                if tt_ + 1 < NT:
                    s5_a(tt_ + 1)
                if tt_ + 2 < NT:
                    s5_mm(tt_ + 2)
                s5_b(tt_)
        if DEBUG:
            for i in range(8):
                dma("sp", dv("dbg_h2T", Dm["dbg_h2T"][:, i, :], i, i + 1), H2T[i], "dbg")

        w_up_v = Dm["w_up"].rearrange("(kc p) n -> p kc n", p=128)

        UPF = view(O_HTC + 8192, [128, 8, 256], BF16)

        def upan(idx):
            return UPF if idx == 0 else UPAN[idx % 3]

        def issue_up(idx):
            hf, j = divmod(idx, NJ)
            s = idx % 3
            if idx == 0:
                dma("pool", UPF[:, :, 0:128], V(w_up_v[:, :, 0:128], [Reg("d_w_up", 0, 1)]), "upf")
                dma("pool", UPF[:, :, 128:256], V(w_up_v[:, :, D_FF:D_FF + 128], [Reg("d_w_up", NJ, NJ + 1)]), "upf")
                return
            dma("pool", UPAN[s][:, :, 0:128], V(w_up_v[:, :, j * 128:(j + 1) * 128], [Reg("d_w_up", j, j + 1)]), "up%d" % s)
            dma("pool", UPAN[s][:, :, 128:256], V(w_up_v[:, :, D_FF + j * 128:D_FF + (j + 1) * 128],
                                                   [Reg("d_w_up", NJ + j, NJ + j + 1)]), "up%d" % s)

        if STOP >= 6:
            for ub_ in UB:
                memset("pool", ub_[:, 0:2], 0.0)
            issue_up(0); issue_up(1)
        pair_ctr = [0]

        def ffn_head(idx):
            hf, j = divmod(idx, NJ)
            t0 = hf * 1024
            if idx + 2 < 2 * NJ:
                issue_up(idx + 2)
            U = upan(idx)
            res = []
            for gv in range(2):
                Pp = bank((pair_ctr[0] % 4) * 2, 2)
                pair_ctr[0] += 1
                for t2 in range(2):
                    for kc in range(8):
                        mm(Pp[:, t2 * 512:(t2 + 1) * 512], U[:, kc, gv * 128:(gv + 1) * 128],
                           H2T[kc][:, t0 + t2 * 512:t0 + (t2 + 1) * 512], kc == 0, kc == 7)
                ub = UB[(2 * idx + gv) % 4]
                ac = ACC[(2 * idx + gv) % 4]
                wc = (gv * NJ + j) * 3
                car = UCAR[:, (gv * NJ + j) * 2:(gv * NJ + j) * 2 + 2]
                act(ub[:, 2:1026], Pp, AF.Copy)
                act(ac, Pp, AF.Copy, scale=FCW[:, wc + 2:wc + 3])
                if hf == 0:
                    act(car, ub[:, 1024:1026], AF.Copy)
                else:
                    act(ub[:, 0:2], car, AF.Copy)
                stt(ac, ub[:, 1:1025], FCW[:, wc + 1:wc + 2], ac, ALU.mult, ALU.add)
                stt(ac, ub[:, 0:1024], FCW[:, wc:wc + 1], ac, ALU.mult, ALU.add)
                res.append(ac)
            return res

        def ffn_tail(idx, res):
            hf, j = divmod(idx, NJ)
            act(res[0], res[0], AF.Silu)
            tt("dve", GT[j], res[0], res[1], ALU.mult)

        carry = None
        for hf in range(2 if STOP >= 6 else 0):
            prev = carry
            for j in range(NJ):
                idx = hf * NJ + j
                if hf == 1 and j == 0:
                    continue
                r = ffn_head(idx)
                if prev is not None:
                    ffn_tail(*prev)
                prev = (idx, r)
            if hf == 0:
                ffn_tail(*prev)
                r = ffn_head(NJ)
                carry = (NJ, r)
            else:
                ffn_tail(*prev)
            bank_ctr[0] = rot[0].index(2 * (pair_ctr[0] % 4))
            SCRY = [view(O_PT, [128, 512], F32), view(O_SCRB, [128, 512], F32)]
            load_x(hf * 8, "x1d"); load_x(hf * 8 + 1, "x1d")
            for t8 in range(8):
                tt_ = hf * 8 + t8
                s = tt_ % 2
                for nh in range(2):
                    P = bank()
                    for j in range(NJ):
                        mm(P, GT[j][:, t8 * 128:(t8 + 1) * 128], WDN[j][:, nh * 512:(nh + 1) * 512], j == 0, j == NJ - 1)
                    yb = SCRY[nh]
                    tt("dve", yb, P, G2B[:, nh * 512:(nh + 1) * 512], ALU.mult)
                    tt("dve", X[s][:, nh * 512:(nh + 1) * 512], yb, X[s][:, nh * 512:(nh + 1) * 512], ALU.add)
                dma("sp", xrows("out", tt_), X[s], "outst")
                if t8 + 2 < 8:
                    load_x(tt_ + 2, "x1d")
        fin_regs = [Reg("d_out", 0, NT)]
        if DEBUG:
            fin_regs += [Reg("d_" + k, 0, 100000) for k in Dm if k.startswith("dbg_")]
        S.add("sp", None, R=[V(None, fin_regs)])

        S.emit(block, {"pe": s_pe, "act": s_act, "dve": s_dve, "pool": s_pool, "sp": s_sp}, dsem_handles)
    return nc


_bf = ml_dtypes.bfloat16


def _consts():
    ident = np.eye(128, dtype=np.float32)
    bones = np.zeros((128, 128), np.float32)
    bones[0:64, 0:64] = 1.0
    bones[64:128, 64:128] = 1.0
    k = np.arange(128)[:, None]
    q = np.arange(128)[None, :]
    mask = np.where(q >= k, 0.0, -30000.0).astype(np.float32)
    sel = np.zeros((128, 8, 128), np.float32)
    for h in range(8):
        for o8 in (0, 32):
            sel[o8 + h, h, :] = 1.0
    return {"k_ident": ident.astype(_bf), "k_bones": bones.astype(_bf), "k_mask": mask.astype(_bf),
            "k_sel": sel.reshape(128, 1024).astype(_bf), "k_identf": np.eye(8, dtype=np.float32),
            "k_ones": np.ones((1, 2048), np.float32).astype(_bf)}


def make_in_maps(x, c, w_ada, b_ada, norm1_g, w_in, b_forget, q_norm_g, k_norm_g,
                 conv_mix_w, w_out, norm2_g, w_up, ffn_conv_w, w_down):
    f = lambda a: np.ascontiguousarray(np.asarray(a, dtype=np.float32))
    x, c, w_ada, b_ada, norm1_g, w_in = f(x), f(c), f(w_ada), f(b_ada), f(norm1_g), f(w_in)
    b_forget, q_norm_g, k_norm_g, conv_mix_w = f(b_forget), f(q_norm_g), f(k_norm_g), f(conv_mix_w)
    w_out, norm2_g, w_up, ffn_conv_w, w_down = f(w_out), f(norm2_g), f(w_up), f(ffn_conv_w), f(w_down)
    fm = lambda v: np.ascontiguousarray(v.reshape(-1, 128).T)
    bfg = np.zeros((104, 1), np.float32)
    for o8 in (0, 32, 64, 96):
        bfg[o8:o8 + 8, 0] = b_forget[0]
    shared = {
        "w_ada": w_ada[0], "b_ada": b_ada[0].reshape(1, -1), "b_fm": fm(b_ada[0]),
        "g1n": fm(norm1_g[0]), "g2n": fm(norm2_g[0]), "w_in": w_in[0], "bfg": bfg,
        "gq": np.tile(q_norm_g[0], 2).reshape(128, 1).copy(), "gk": np.tile(k_norm_g[0], 2).reshape(128, 1).copy(),
        "cw": np.ascontiguousarray(conv_mix_w[0].reshape(3, 4, 128).transpose(2, 1, 0).reshape(128, 12)),
        "w_out": w_out[0], "w_up": w_up[0],
        "fcw": np.ascontiguousarray(ffn_conv_w[0].reshape(3, 44, 128).transpose(2, 1, 0).reshape(128, 132)),
        "w_down": w_down[0],
    }
    shared.update(_consts())
    maps = []
    for b in range(x.shape[0]):
        m = dict(shared)
        m["x"] = x[b]
        m["c"] = np.ascontiguousarray(c[b].reshape(128, 8))
        maps.append(m)
    return maps


def kernel(x, c, w_ada, b_ada, norm1_g, w_in, b_forget, q_norm_g, k_norm_g,
           conv_mix_w, w_out, norm2_g, w_up, ffn_conv_w, w_down):
    in_maps = make_in_maps(x, c, w_ada, b_ada, norm1_g, w_in, b_forget, q_norm_g, k_norm_g,
                           conv_mix_w, w_out, norm2_g, w_up, ffn_conv_w, w_down)
    nc = build_nc()
    res = run_bass_kernel_spmd(nc, in_maps, core_ids=list(range(len(in_maps))))
    out = np.stack([np.asarray(r["out"], dtype=np.float32) for r in res.results], axis=0)
    return out
```
